# Optimizing a Trainium2 kernel written in Bass

```python
import math
import jax
import jax.numpy as jnp
from jax import lax
import numpy as np

D_MODEL = 1024
BATCH = 4
SEQ = 8192
DEPTH = 4

GRID_W = 64
CTX_LEN = 256
N_MIXERS = 3
N_LAYERS_A = (DEPTH + 2) // 3
N_LAYERS_B = (DEPTH + 1) // 3
N_LAYERS_C = DEPTH // 3

D_FF = 2816
RMS_EPS = 1e-6
N_MOD = 9

S5_CH = 16
S5_GROUPS = D_MODEL // S5_CH
S5_STATE = 64
S5_DT_MIN = 1e-3
S5_DT_MAX = 1e-1

ATT_HEAD_DIM = 128
ATT_HEADS = D_MODEL // ATT_HEAD_DIM
ATT_KV_HEADS = 2
ATT_GROUP = ATT_HEADS // ATT_KV_HEADS
ATT_BLOCK = 128
ROPE_THETA = 10000.0
ROPE_AXIS_DIM = ATT_HEAD_DIM // 2

GDN_HEAD_DIM = 128
GDN_HEADS = D_MODEL // GDN_HEAD_DIM
GDN_CHUNK = 64
GDN_CONV = 3
GDN_PROJ = 4 * D_MODEL + 4 * GDN_HEADS

kernel_name = "hybrid_s5_gqa_gdn_macaron_dit"


def rms_norm(x, g):
    x32 = x.astype(jnp.float32)
    y = x32 * lax.rsqrt(jnp.mean(x32 * x32, axis=-1, keepdims=True) + RMS_EPS)
    return (y * g.astype(jnp.float32)).astype(x.dtype)


def modulate(x, g, mod, j):
    return rms_norm(x, g) * (1 + mod[:, :, 3 * j + 1]) + mod[:, :, 3 * j]


def swiglu(h, wi, wo):
    gate, up = jnp.split(h @ wi, 2, axis=-1)
    return (jax.nn.silu(gate) * up) @ wo


def ffn_sublayer(xs, g, mod, j, wi, wo):
    return xs + 0.5 * mod[:, :, 3 * j + 2] * swiglu(modulate(xs, g, mod, j), wi, wo)


def axial_rope_tables(n_tokens):
    rows = n_tokens // GRID_W
    row = jnp.repeat(jnp.arange(rows, dtype=jnp.float32), GRID_W)
    col = jnp.tile(jnp.arange(GRID_W, dtype=jnp.float32), rows)
    inv = ROPE_THETA ** (-jnp.arange(0, ROPE_AXIS_DIM, 2, dtype=jnp.float32) / ROPE_AXIS_DIM)
    ang_r = row[:, None] * inv[None, :]
    ang_c = col[:, None] * inv[None, :]
    return (jnp.cos(ang_r), jnp.sin(ang_r), jnp.cos(ang_c), jnp.sin(ang_c))


def rope_half(x, cos, sin):
    x1, x2 = jnp.split(x, 2, axis=-1)
    cos = cos[None, :, None, :].astype(x.dtype)
    sin = sin[None, :, None, :].astype(x.dtype)
    return jnp.concatenate([x1 * cos - x2 * sin, x2 * cos + x1 * sin], axis=-1)


def apply_axial_rope(x, tables):
    cr, sr, cc, sc = tables
    xr, xc = jnp.split(x, 2, axis=-1)
    return jnp.concatenate([rope_half(xr, cr, sr), rope_half(xc, cc, sc)], axis=-1)


def s5_combine(e1, e2):
    a1r, a1i, b1r, b1i = e1
    a2r, a2i, b2r, b2i = e2
    return (a2r * a1r - a2i * a1i,
            a2r * a1i + a2i * a1r,
            a2r * b1r - a2i * b1i + b2r,
            a2r * b1i + a2i * b1r + b2i)


def s5_direction(u, lam_re, lam_im, log_dt, b_re, b_im, c_re, c_im, h0, reverse, readout):
    f32 = jnp.float32
    lam_re, lam_im = lam_re.astype(f32), lam_im.astype(f32)
    dt = jnp.exp(log_dt.astype(f32))[:, None]
    mag = jnp.exp(lam_re * dt)
    ar = mag * jnp.cos(lam_im * dt)
    ai = mag * jnp.sin(lam_im * dt)
    den = lam_re * lam_re + lam_im * lam_im
    fr = ((ar - 1) * lam_re + ai * lam_im) / den
    fi = (ai * lam_re - (ar - 1) * lam_im) / den
    b_re, b_im = b_re.astype(f32), b_im.astype(f32)
    bb_re = fr[..., None] * b_re - fi[..., None] * b_im
    bb_im = fr[..., None] * b_im + fi[..., None] * b_re
    bu_re = jnp.einsum('blgh,gph->blgp', u, bb_re)
    bu_im = jnp.einsum('blgh,gph->blgp', u, bb_im)
    if h0 is not None:
        h0r, h0i = h0
        first = -1 if reverse else 0
        bu_re = bu_re.at[:, first].add(ar * h0r - ai * h0i)
        bu_im = bu_im.at[:, first].add(ar * h0i + ai * h0r)
    n = u.shape[1]
    a_re = jnp.broadcast_to(ar, (1, n) + ar.shape)
    a_im = jnp.broadcast_to(ai, (1, n) + ai.shape)
    _, _, sr, si = lax.associative_scan(s5_combine, (a_re, a_im, bu_re, bu_im), reverse=reverse, axis=1)
    last = 0 if reverse else -1
    final = (sr[:, last], si[:, last])
    y = None
    if readout:
        y = (jnp.einsum('blgp,ghp->blgh', sr, c_re.astype(f32))
             - jnp.einsum('blgp,ghp->blgh', si, c_im.astype(f32)))
    return y, final


def s5_mixer(hc, hl, lam_re, lam_im, log_dt, b_re, b_im, c_re, c_im, d_skip, w_glu, need_ctx):
    def groups(h):
        return h.astype(jnp.float32).reshape(h.shape[0], h.shape[1], S5_GROUPS, S5_CH)

    def direction(u, r, h0, reverse, readout):
        return s5_direction(u, lam_re[r], lam_im[r], log_dt[r], b_re[r], b_im[r],
                            c_re[r], c_im[r], h0, reverse, readout)

    def finish(u, y_f, y_b, dtype):
        y = y_f + y_b + d_skip.astype(jnp.float32).reshape(S5_GROUPS, S5_CH) * u
        y = jax.nn.gelu(y.reshape(u.shape[0], u.shape[1], D_MODEL)).astype(dtype)
        z = y @ w_glu
        return z[..., :D_MODEL] * jax.nn.sigmoid(z[..., D_MODEL:])

    uc, ul = groups(hc), groups(hl)
    yc_f, hcf = direction(uc, 0, None, False, need_ctx)
    yc_b, hcb = direction(uc, 1, None, True, need_ctx)
    yl_f, _ = direction(ul, 0, hcf, False, True)
    yl_b, _ = direction(ul, 1, hcb, True, True)
    yl = finish(ul, yl_f, yl_b, hl.dtype)
    yc = finish(uc, yc_f, yc_b, hc.dtype) if need_ctx else None
    return yc, yl


def gqa_attend(q, k, v):
    s = jnp.einsum('bqkgd,bskd->bkgqs', q, k, preferred_element_type=jnp.float32) * (ATT_HEAD_DIM ** -0.5)
    p = jax.nn.softmax(s, axis=-1).astype(v.dtype)
    return jnp.einsum('bkgqs,bskd->bqkgd', p, v)


def attention_mixer(hc, hl, w_qkv, q_gain, k_gain, w_o, rope, need_ctx):
    def project(h):
        b, n, _ = h.shape
        q, k, v = jnp.split(h @ w_qkv, [ATT_HEADS * ATT_HEAD_DIM, (ATT_HEADS + ATT_KV_HEADS) * ATT_HEAD_DIM], axis=-1)
        q = rms_norm(q.reshape(b, n, ATT_HEADS, ATT_HEAD_DIM), q_gain)
        k = rms_norm(k.reshape(b, n, ATT_KV_HEADS, ATT_HEAD_DIM), k_gain)
        v = v.reshape(b, n, ATT_KV_HEADS, ATT_HEAD_DIM)
        return q, k, v

    qc, kc, vc = project(hc)
    ql, kl, vl = project(hl)
    ql = apply_axial_rope(ql, rope)
    kl = apply_axial_rope(kl, rope)
    b, n, _ = hl.shape
    k_all = jnp.concatenate([kc, kl], axis=1)
    v_all = jnp.concatenate([vc, vl], axis=1)
    nb = n // ATT_BLOCK
    q_blocks = jnp.moveaxis(ql.reshape(b, nb, ATT_BLOCK, ATT_KV_HEADS, ATT_GROUP, ATT_HEAD_DIM), 1, 0)
    o = lax.map(lambda qb: gqa_attend(qb, k_all, v_all), q_blocks)
    yl = jnp.moveaxis(o, 0, 1).reshape(b, n, D_MODEL) @ w_o
    yc = None
    if need_ctx:
        nc = hc.shape[1]
        oc = gqa_attend(qc.reshape(b, nc, ATT_KV_HEADS, ATT_GROUP, ATT_HEAD_DIM), kc, vc)
        yc = oc.reshape(b, nc, D_MODEL) @ w_o
    return yc, yl


def short_conv(x, w):
    ch = x.shape[-1]
    return lax.conv_general_dilated(x, w[:, None, :].astype(x.dtype), window_strides=(1,),
                                    padding=[(GDN_CONV // 2, GDN_CONV // 2)],
                                    dimension_numbers=('NWC', 'WIO', 'NWC'),
                                    feature_group_count=ch)


def l2_norm(x):
    return x * lax.rsqrt(jnp.sum(x * x, axis=-1, keepdims=True) + 1e-6)


def chunk_gated_delta(q, k, v, g, beta, s0):
    b, n, h, _ = q.shape
    dv = v.shape[-1]
    nc = n // GDN_CHUNK

    def chunks(t):
        t = t.reshape((b, nc, GDN_CHUNK, h) + t.shape[3:])
        return jnp.moveaxis(t, (1, 3), (0, 2))

    qc, kc, vc = chunks(q), chunks(k), chunks(v)
    gc = jnp.cumsum(chunks(g), axis=-1)
    bc = chunks(beta)
    idx = jnp.arange(GDN_CHUNK)
    incl = idx[:, None] >= idx[None, :]
    strict = idx[:, None] > idx[None, :]
    diff = gc[..., :, None] - gc[..., None, :]
    decay = jnp.where(incl, jnp.exp(jnp.where(incl, diff, 0.0)), 0.0)
    kk = jnp.einsum('nbhid,nbhjd->nbhij', kc, kc)
    a = jnp.where(strict, bc[..., :, None] * kk * decay, 0.0)
    m = a + jnp.eye(GDN_CHUNK, dtype=a.dtype)
    rhs = jnp.concatenate([vc * bc[..., None], kc * (bc * jnp.exp(gc))[..., None]], axis=-1)
    sol = lax.linalg.triangular_solve(m, rhs, left_side=True, lower=True, unit_diagonal=True)
    u, w = sol[..., :dv], sol[..., dv:]
    qk = jnp.einsum('nbhid,nbhjd->nbhij', qc, kc) * decay
    q_dec = qc * jnp.exp(gc)[..., None]
    k_dec = kc * jnp.exp(gc[..., -1:] - gc)[..., None]
    g_end = jnp.exp(gc[..., -1])

    def step(s, inp):
        u_n, w_n, q_n, k_n, qk_n, g_n = inp
        v_new = u_n - jnp.einsum('bhcd,bhde->bhce', w_n, s)
        o_n = jnp.einsum('bhcd,bhde->bhce', q_n, s) + jnp.einsum('bhij,bhje->bhie', qk_n, v_new)
        s = g_n[..., None, None] * s + jnp.einsum('bhcd,bhce->bhde', k_n, v_new)
        return s, o_n

    s_fin, o = lax.scan(step, s0, (u, w, q_dec, k_dec, qk, g_end))
    o = jnp.moveaxis(o, (0, 2), (1, 3)).reshape(b, n, h, dv)
    return o, s_fin


def gdn_mixer(hc, hl, w_in, conv_w, a_log, dt_bias, o_gain, w_o, need_ctx):
    f32 = jnp.float32

    def prep(h):
        b, n, _ = h.shape
        proj = h @ w_in
        qkv = jax.nn.silu(short_conv(proj[..., :3 * D_MODEL], conv_w)).astype(f32)
        qkv = qkv.reshape(b, n, 3, GDN_HEADS, GDN_HEAD_DIM)
        q = l2_norm(qkv[:, :, 0]) * (GDN_HEAD_DIM ** -0.5)
        k = l2_norm(qkv[:, :, 1])
        v = qkv[:, :, 2]
        z = proj[..., 3 * D_MODEL:4 * D_MODEL]
        ab = proj[..., 4 * D_MODEL:].astype(f32).reshape(b, n, 2, 2, GDN_HEADS)
        g = -jnp.exp(a_log.astype(f32)) * jax.nn.softplus(ab[:, :, 0] + dt_bias.astype(f32))
        beta = jax.nn.sigmoid(ab[:, :, 1])
        return q, k, v, g, beta, z

    def run(t, r, s0, reverse):
        q, k, v, g, beta = t[0], t[1], t[2], t[3][:, :, r], t[4][:, :, r]
        if reverse:
            q, k, v, g, beta = (jnp.flip(a, axis=1) for a in (q, k, v, g, beta))
        o, s = chunk_gated_delta(q, k, v, g, beta, s0)
        return (jnp.flip(o, axis=1) if reverse else o), s

    def finish(o, z, dtype):
        b, n = o.shape[0], o.shape[1]
        zh = z.reshape(b, n, GDN_HEADS, GDN_HEAD_DIM).astype(f32)
        y = rms_norm(o, o_gain) * jax.nn.silu(zh)
        return y.reshape(b, n, D_MODEL).astype(dtype) @ w_o

    tc, tl = prep(hc), prep(hl)
    s0 = jnp.zeros((hl.shape[0], GDN_HEADS, GDN_HEAD_DIM, GDN_HEAD_DIM), f32)
    oc_f, sc_f = run(tc, 0, s0, False)
    oc_b, sc_b = run(tc, 1, s0, True)
    ol_f, _ = run(tl, 0, sc_f, False)
    ol_b, _ = run(tl, 1, sc_b, True)
    yl = finish(ol_f + ol_b, tl[5], hl.dtype)
    yc = finish(oc_f + oc_b, tc[5], hc.dtype) if need_ctx else None
    return yc, yl


def setup_inputs(seed: int = 0) -> dict:
    key = jax.random.key(seed)
    ks = iter(jax.random.split(key, 40))
    f32 = jnp.float32

    def nrm(shape, scale=1.0):
        return scale * jax.random.normal(next(ks), shape, f32)

    def unif(shape, lo, hi):
        return jax.random.uniform(next(ks), shape, f32, minval=lo, maxval=hi)

    D = D_MODEL
    inv_d = D ** -0.5
    dt_g = jnp.exp(unif((N_LAYERS_C, 2, GDN_HEADS), math.log(1e-3), math.log(1e-1)))
    return {
        "x": nrm((BATCH, SEQ, D)),
        "c": nrm((BATCH, D)),
        "ctx": nrm((BATCH, CTX_LEN, D)),
        "c_ctx": nrm((D,)),
        "norm_g": 1.0 + nrm((DEPTH, 3, D), 0.02),
        "ada_w": nrm((DEPTH, D, N_MOD * D), 0.5 * inv_d),
        "ada_b": nrm((DEPTH, N_MOD * D), 0.02),
        "ffn_wi": nrm((DEPTH, 2, D, 2 * D_FF), inv_d),
        "ffn_wo": nrm((DEPTH, 2, D_FF, D), D_FF ** -0.5),
        "s5_lam_re": -0.5 + nrm((N_LAYERS_A, 2, S5_GROUPS, S5_STATE), 0.01),
        "s5_lam_im": math.pi * jnp.arange(S5_STATE, dtype=f32) + nrm((N_LAYERS_A, 2, S5_GROUPS, S5_STATE), 0.01),
        "s5_log_dt": unif((N_LAYERS_A, 2, S5_GROUPS), math.log(S5_DT_MIN), math.log(S5_DT_MAX)),
        "s5_b_re": nrm((N_LAYERS_A, 2, S5_GROUPS, S5_STATE, S5_CH), (2 * S5_CH) ** -0.5),
        "s5_b_im": nrm((N_LAYERS_A, 2, S5_GROUPS, S5_STATE, S5_CH), (2 * S5_CH) ** -0.5),
        "s5_c_re": nrm((N_LAYERS_A, 2, S5_GROUPS, S5_CH, S5_STATE), S5_STATE ** -0.5),
        "s5_c_im": nrm((N_LAYERS_A, 2, S5_GROUPS, S5_CH, S5_STATE), S5_STATE ** -0.5),
        "s5_d": nrm((N_LAYERS_A, D)),
        "s5_w_glu": nrm((N_LAYERS_A, D, 2 * D), inv_d),
        "attn_w_qkv": nrm((N_LAYERS_B, D, (ATT_HEADS + 2 * ATT_KV_HEADS) * ATT_HEAD_DIM), inv_d),
        "attn_q_gain": 1.0 + nrm((N_LAYERS_B, ATT_HEAD_DIM), 0.02),
        "attn_k_gain": 1.0 + nrm((N_LAYERS_B, ATT_HEAD_DIM), 0.02),
        "attn_w_o": nrm((N_LAYERS_B, ATT_HEADS * ATT_HEAD_DIM, D), (ATT_HEADS * ATT_HEAD_DIM) ** -0.5),
        "gdn_w_in": nrm((N_LAYERS_C, D, GDN_PROJ), inv_d),
        "gdn_conv_w": nrm((N_LAYERS_C, GDN_CONV, 3 * D), GDN_CONV ** -0.5),
        "gdn_a_log": jnp.log(unif((N_LAYERS_C, 2, GDN_HEADS), 1.0, 16.0)),
        "gdn_dt_bias": dt_g + jnp.log(-jnp.expm1(-dt_g)),
        "gdn_o_gain": 1.0 + nrm((N_LAYERS_C, GDN_HEAD_DIM), 0.02),
        "gdn_w_o": nrm((N_LAYERS_C, D, D), inv_d),
    }


def reference(x, c, ctx, c_ctx, norm_g, ada_w, ada_b, ffn_wi, ffn_wo,
              s5_lam_re, s5_lam_im, s5_log_dt, s5_b_re, s5_b_im, s5_c_re, s5_c_im, s5_d, s5_w_glu,
              attn_w_qkv, attn_q_gain, attn_k_gain, attn_w_o,
              gdn_w_in, gdn_conv_w, gdn_a_log, gdn_dt_bias, gdn_o_gain, gdn_w_o):
    rope = axial_rope_tables(x.shape[1])
    sc_l = jax.nn.silu(c)[:, None, :]
    sc_c = jax.nn.silu(c_ctx)[None, None, :]
    xl, xc = x, ctx
    for i in range(DEPTH):
        need_ctx = i < DEPTH - 1
        mod_l = (sc_l @ ada_w[i] + ada_b[i]).reshape(sc_l.shape[0], 1, N_MOD, D_MODEL)
        mod_c = (sc_c @ ada_w[i] + ada_b[i]).reshape(1, 1, N_MOD, D_MODEL)

        xl = ffn_sublayer(xl, norm_g[i, 0], mod_l, 0, ffn_wi[i, 0], ffn_wo[i, 0])
        xc = ffn_sublayer(xc, norm_g[i, 0], mod_c, 0, ffn_wi[i, 0], ffn_wo[i, 0])

        hl = modulate(xl, norm_g[i, 1], mod_l, 1)
        hc = modulate(xc, norm_g[i, 1], mod_c, 1)
        kind, slot = i % N_MIXERS, i // N_MIXERS
        if kind == 0:
            yc, yl = s5_mixer(hc, hl, s5_lam_re[slot], s5_lam_im[slot], s5_log_dt[slot],
                              s5_b_re[slot], s5_b_im[slot], s5_c_re[slot], s5_c_im[slot],
                              s5_d[slot], s5_w_glu[slot], need_ctx)
        elif kind == 1:
            yc, yl = attention_mixer(hc, hl, attn_w_qkv[slot], attn_q_gain[slot], attn_k_gain[slot],
                                     attn_w_o[slot], rope, need_ctx)
        else:
            yc, yl = gdn_mixer(hc, hl, gdn_w_in[slot], gdn_conv_w[slot], gdn_a_log[slot],
                               gdn_dt_bias[slot], gdn_o_gain[slot], gdn_w_o[slot], need_ctx)
        xl = xl + mod_l[:, :, 5] * yl

        if need_ctx:
            xc = xc + mod_c[:, :, 5] * yc
            xc = ffn_sublayer(xc, norm_g[i, 2], mod_c, 2, ffn_wi[i, 1], ffn_wo[i, 1])
        xl = ffn_sublayer(xl, norm_g[i, 2], mod_l, 2, ffn_wi[i, 1], ffn_wo[i, 1])
    return xl
```

```python
import numpy as np
from contextlib import ExitStack
import concourse.bass as bass
import concourse.mybir as mybir
from concourse.bass_utils import run_bass_kernel_spmd

F32 = mybir.dt.float32
BF16 = mybir.dt.bfloat16
ALU = mybir.AluOpType
AF = mybir.ActivationFunctionType
AX = mybir.AxisListType

EPOCH = 16000


class Tile:
    def __init__(self, prog, name, shape, dtype, space):
        self.prog = prog
        self.name = name
        self.shape = shape
        self.dtype = dtype
        self.space = space
        self.t = None
        self.last_w = None
        self.readers = {}
        self.sem = None
        self.dma_count = 0

    def __getitem__(self, idx):
        return View(self, self.t[idx])

    @property
    def v(self):
        return View(self, self.t[:])


class View:
    def __init__(self, tile, ap):
        self.tile = tile
        self.ap = ap

    def __getitem__(self, idx):
        return View(self.tile, self.ap[idx])

    def rearrange(self, *a, **k):
        return View(self.tile, self.ap.rearrange(*a, **k))

    def bitcast(self, dt):
        return View(self.tile, self.ap.bitcast(dt))


def _ap(x):
    if isinstance(x, View):
        return x.ap
    if isinstance(x, Tile):
        return x.t[:]
    return x


def _tile(x):
    if isinstance(x, View):
        return x.tile
    if isinstance(x, Tile):
        return x
    return None


class Prog:
    ENGS = ("pe", "dve", "act", "pool", "sp")

    def __init__(self, nc, stack):
        self.nc = nc
        self.stack = stack
        self.ops = {e: [] for e in self.ENGS}
        self.count = {e: 0 for e in self.ENGS}
        self.esems = {e: [] for e in self.ENGS}
        self.known = {e: {} for e in self.ENGS}
        self.tiles = []
        self.nsem = 0
        self.out_dma_tiles = []
        self.pending_waits = {e: [] for e in self.ENGS}

    def sb(self, name, shape, dtype=F32):
        t = Tile(self, name, shape, dtype, "sb")
        t.t = self.stack.enter_context(self.nc.sbuf_tensor(name, list(shape), dtype))
        self.tiles.append(t)
        return t

    def ps(self, name, shape, dtype=F32):
        t = Tile(self, name, shape, dtype, "ps")
        t.t = self.stack.enter_context(self.nc.psum_tensor(name, list(shape), dtype))
        self.tiles.append(t)
        return t

    def _newsem(self, name):
        self.nsem += 1
        return self.stack.enter_context(self.nc.semaphore(name))

    def _esem(self, eng, epoch):
        while len(self.esems[eng]) <= epoch:
            self.esems[eng].append(self._newsem(f"s_{eng}_{len(self.esems[eng])}"))
        return self.esems[eng][epoch]

    def _deps(self, eng, reads, writes):
        deps = []
        for x in reads:
            t = _tile(x)
            if t is None:
                continue
            if t.last_w is not None:
                deps.append(t.last_w)
            if t.space == "ps":
                for ev in t.readers.values():
                    if not (ev[0] == "eng" and ev[1] == eng):
                        deps.append(ev)
        for x in writes:
            t = _tile(x)
            if t is None:
                continue
            if t.last_w is not None:
                deps.append(t.last_w)
            deps.extend(t.readers.values())
        waits = []
        kn = self.known[eng]
        for d in deps:
            if d[0] == "eng":
                _, e, idx = d
                if e == eng and eng == "pe":
                    continue
                epoch, k = divmod(idx, EPOCH)
                sem = self._esem(e, epoch)
                val = k + 1
            else:
                _, t, cnt = d
                sem = t.sem
                val = 16 * cnt
            key = id(sem)
            if kn.get(key, 0) >= val:
                continue
            kn[key] = val
            waits.append((sem, val))
        best = {}
        for s, v in waits:
            if id(s) not in best or best[id(s)][1] < v:
                best[id(s)] = (s, v)
        return list(best.values())

    def _record(self, ev, reads, writes):
        for x in writes:
            t = _tile(x)
            if t is None:
                continue
            t.last_w = ev
            t.readers = {}
        for x in reads:
            t = _tile(x)
            if t is None:
                continue
            key = (ev[0], ev[1]) if ev[0] == "eng" else ("dma", id(ev[1]))
            t.readers[key] = ev

    def op(self, eng, fn, reads, writes):
        waits = self._deps(eng, reads, writes)
        idx = self.count[eng]
        self.count[eng] += 1
        epoch, k = divmod(idx, EPOCH)
        sem = self._esem(eng, epoch)
        waits = self.pending_waits[eng] + waits
        self.pending_waits[eng] = []
        self.ops[eng].append((waits, fn, sem, 1))
        self._record(("eng", eng, idx), reads, writes)

    def dma(self, out, in_, eng="sp", **kw):
        to, ti = _tile(out), _tile(in_)
        owner = to if to is not None else ti
        assert owner is not None
        if owner.sem is None:
            owner.sem = self._newsem(f"d_{owner.name}")
        reads = [in_] if ti is not None else []
        writes = [out] if to is not None else []
        waits = self._deps(eng, reads, writes)
        owner.dma_count += 1
        o, i = _ap(out), _ap(in_)
        waits = self.pending_waits[eng] + waits
        self.pending_waits[eng] = []
        self.ops[eng].append((waits, lambda e: e.dma_start(out=o, in_=i, **kw), owner.sem, 16))
        self._record(("dma", owner, owner.dma_count), reads, writes)
        if to is None and owner not in self.out_dma_tiles:
            self.out_dma_tiles.append(owner)

    def barrier(self):
        for eng in self.ENGS:
            kn = self.known[eng]
            waits = []
            for e in self.ENGS:
                n = self.count[e]
                if n == 0 or e == "sp":
                    continue
                epoch, k = divmod(n - 1, EPOCH)
                sem = self._esem(e, epoch)
                if kn.get(id(sem), 0) < k + 1:
                    kn[id(sem)] = k + 1
                    waits.append((sem, k + 1))
            for t in self.tiles:
                if t.sem is not None and t.dma_count > 0:
                    v = 16 * t.dma_count
                    if kn.get(id(t.sem), 0) < v:
                        kn[id(t.sem)] = v
                        waits.append((t.sem, v))
            if waits:
                self.pending_waits[eng].extend(waits)

    def mm(self, out, lhsT, rhs, start=True, stop=True, **kw):
        o, l, r = _ap(out), _ap(lhsT), _ap(rhs)
        self.op("pe", lambda e: e.matmul(o, l, r, start=start, stop=stop, **kw), [lhsT, rhs], [out])

    def transpose(self, out, in_, ident):
        o, i, d = _ap(out), _ap(in_), _ap(ident)
        self.op("pe", lambda e: e.transpose(o, i, d), [in_, ident], [out])

    def act(self, out, in_, func, bias=None, scale=None, accum_out=None, eng="act"):
        o, i = _ap(out), _ap(in_)
        kw = {}
        reads = [in_]
        writes = [out]
        if bias is not None:
            kw["bias"] = _ap(bias)
            reads.append(bias)
        if scale is not None:
            kw["scale"] = _ap(scale)
            reads.append(scale)
        if accum_out is not None:
            kw["accum_out"] = _ap(accum_out)
            writes.append(accum_out)
        self.op("act", lambda e: e.activation(o, i, func, **kw), reads, writes)

    def tt(self, out, in0, in1, op, eng="dve"):
        o, a, b = _ap(out), _ap(in0), _ap(in1)
        self.op(eng, lambda e: e.tensor_tensor(o, a, b, op), [in0, in1], [out])

    def ts(self, out, in0, s1, s2=None, op0=ALU.mult, op1=None, eng="dve", accum_out=None):
        o, a = _ap(out), _ap(in0)
        reads = [in0]
        writes = [out]
        s1a, s2a = _ap(s1), _ap(s2)
        if _tile(s1) is not None:
            reads.append(s1)
        if _tile(s2) is not None:
            reads.append(s2)
        kw = {}
        if op1 is not None:
            kw["op1"] = op1
        if accum_out is not None:
            kw["accum_out"] = _ap(accum_out)
            writes.append(accum_out)
        self.op(eng, lambda e: e.tensor_scalar(o, a, s1a, s2a, op0, **kw), reads, writes)

    def stt(self, out, in0, scalar, in1, op0, op1, eng="dve"):
        o, a, b = _ap(out), _ap(in0), _ap(in1)
        s = _ap(scalar)
        reads = [in0, in1]
        if _tile(scalar) is not None:
            reads.append(scalar)
        self.op(eng, lambda e: e.scalar_tensor_tensor(o, a, s, b, op0, op1), reads, [out])

    def copy(self, out, in_, eng="dve"):
        o, i = _ap(out), _ap(in_)
        if eng == "act":
            self.op("act", lambda e: e.activation(o, i, AF.Copy), [in_], [out])
        else:
            self.op(eng, lambda e: e.tensor_copy(o, i), [in_], [out])

    def memset(self, out, val, eng="dve"):
        o = _ap(out)
        self.op(eng, lambda e: e.memset(o, val), [], [out])

    def reduce(self, out, in_, op, axis=AX.X, eng="dve"):
        o, i = _ap(out), _ap(in_)
        self.op(eng, lambda e: e.tensor_reduce(o, i, axis, op), [in_], [out])

    def emit(self):
        nc = self.nc
        final_waits = []
        for t in self.tiles:
            if t.sem is not None and t.dma_count > 0:
                final_waits.append((t.sem, 16 * t.dma_count))
        ops = self.ops
        engmap = {"pe": "tensor", "dve": "vector", "act": "scalar", "pool": "gpsimd", "sp": "sync"}
        with nc.Block() as block:
            for ename, bname in engmap.items():
                lst = ops[ename]
                fw = final_waits if ename == "sp" else []

                def body(eng, lst=lst, fw=fw):
                    for waits, fn, sem, inc in lst:
                        for s, v in waits:
                            eng.wait_ge(s, v)
                        fn(eng).then_inc(sem, inc)
                    for s, v in fw:
                        eng.wait_ge(s, v)

                if lst or fw:
                    getattr(block, bname)(body)

D = 1024
NMOD = 9
DEPTH = 4
DFF = 2816


def make_ident(p, name="ident"):
    ident = p.sb(name, [128, 128])
    p.memset(ident, 0.0, eng="pool")
    ia = ident.t[:]
    p.op("pool", lambda e: e.affine_select(ia, ia, [[-1, 128]], ALU.not_equal, 1.0, base=0, channel_multiplier=1),
         [ident], [ident])
    return ident


def build_mod():
    nc = bass.Bass("TRN2", target_bir_lowering=False)
    cc = nc.dram_tensor("cc", [8, D], F32, kind="ExternalInput").ap()
    w = nc.dram_tensor("w", [D, 4608], F32, kind="ExternalInput").ap()
    b = nc.dram_tensor("b", [1, 4608], F32, kind="ExternalInput").ap()
    out = nc.dram_tensor("mod", [8, 4608], F32, kind="ExternalOutput").ap()
    with ExitStack() as st:
        p = Prog(nc, st)
        ident = make_ident(p)
        c_sb = p.sb("c_sb", [8, D])
        s_sb = p.sb("s_sb", [8, D])
        scT = p.sb("scT", [128, 8, 8])
        pT = p.ps("pT", [128, 512])
        p.dma(c_sb, cc)
        p.act(s_sb, c_sb, AF.Silu)
        for k in range(8):
            p.transpose(pT[:, k * 8:(k + 1) * 8], s_sb[:, k * 128:(k + 1) * 128], ident[0:8, 0:8])
        p.copy(scT.v.rearrange("p k m -> p (k m)"), pT[:, 0:64])
        wts = [p.sb(f"w{i}", [128, 8, 512]) for i in range(2)]
        bts = [p.sb(f"b{i}", [8, 512]) for i in range(2)]
        pos = [p.ps(f"po{i}", [128, 512]) for i in range(2)]
        ots = [p.sb(f"o{i}", [8, 512]) for i in range(2)]
        for j in range(9):
            wt, bt, po, ot = wts[j % 2], bts[j % 2], pos[j % 2], ots[j % 2]
            p.dma(wt, w[:, j * 512:(j + 1) * 512].rearrange("(k p) n -> p k n", p=128))
            p.dma(bt, b[:, j * 512:(j + 1) * 512].partition_broadcast(8), eng="act")
            for k in range(8):
                p.mm(po[0:8, :], scT[:, k, :], wt[:, k, :], start=(k == 0), stop=(k == 7))
            p.tt(ot, po[0:8, :], bt, ALU.add)
            p.dma(out[:, j * 512:(j + 1) * 512], ot)
        p.emit()
    return nc


def run_mod(inp):
    cc = np.zeros((8, D), np.float32)
    cc[0:4] = inp["c"]
    cc[4] = inp["c_ctx"]
    in_maps = []
    for core in range(8):
        l, h = core // 2, core % 2
        in_maps.append({
            "cc": cc,
            "w": np.ascontiguousarray(inp["ada_w"][l][:, h * 4608:(h + 1) * 4608]),
            "b": np.ascontiguousarray(inp["ada_b"][l][None, h * 4608:(h + 1) * 4608]),
        })
    res = run_bass_kernel_spmd(build_mod(), in_maps, core_ids=list(range(8)))
    mod = np.zeros((DEPTH, 5, NMOD, D), np.float32)
    for core in range(8):
        l, h = core // 2, core % 2
        m = res.results[core]["mod"][0:5]
        mod[l].reshape(5, NMOD * D)[:, h * 4608:(h + 1) * 4608] = m
    return mod

GT = 256
RMS_EPS = 1e-6


class TCtx:
    ARENA = 50500

    def __init__(self, p, nc):
        self.p = p
        self.nc = nc
        self.ident = make_ident(p)
        self.eps = p.sb("eps_c", [128, 1])
        p.memset(self.eps, RMS_EPS)
        self.arena = p.sb("arena", [128, self.ARENA])
        self.pT = [p.ps(f"pT{i}", [128, 512]) for i in range(2)]
        self.pA = [p.ps(f"pA{i}", [128, 512]) for i in range(4)]
        self.pO = [p.ps(f"pO{i}", [128, 512]) for i in range(2)]
        self.npass = 0
        self.off = 0

    def alloc(self, name, shape, dtype=F32):
        p = self.p
        per = 1
        for s_ in shape[1:]:
            per *= s_
        nbytes = per * (2 if dtype == BF16 else 4)
        n4 = (nbytes + 3) // 4
        assert self.off + n4 <= self.ARENA, (name, self.off, n4)
        ap = self.arena.t[0:shape[0], self.off:self.off + n4]
        self.off += n4
        if dtype == BF16:
            ap = ap.bitcast(BF16)
            ap = ap[:, 0:per]
        if len(shape) == 3:
            ap = ap.rearrange("p (a b) -> p a b", a=shape[1])
        t = Tile(p, f"{name}_{self.npass}", shape, dtype, "sb")
        t.t = ap
        p.tiles.append(t)
        return t

    def begin_pass(self, wA_cols=0, wB_k=0):
        p = self.p
        if self.npass > 0:
            p.barrier()
        self.npass += 1
        self.off = 0
        if wA_cols:
            self.wA = self.alloc("wA", [128, 8, wA_cols], BF16)
        if wB_k:
            self.wB = self.alloc("wB", [128, wB_k, 1024], BF16)
        self.Gp = self.alloc("Gp", [128, D])
        self.SH = self.alloc("SH", [128, D])
        self.GA = self.alloc("GA", [128, D])
        self.xt = [self.alloc(f"xt{i}", [128, D]) for i in range(4)]
        self.h = [self.alloc(f"h{i}", [128, D]) for i in range(2)]
        self.junk = self.alloc("junk", [128, D], BF16)
        self.hT4 = [self.alloc(f"hT{i}", [128, 2, GT], BF16) for i in range(4)]
        self.ss = [self.alloc(f"ss{i}", [128, 1]) for i in range(2)]
        self.rs = [self.alloc(f"rs{i}", [128, 1]) for i in range(2)]
        self.tmp = [self.alloc(f"tmp{i}", [128, 512]) for i in range(2)]

    def load_w(self, dst_view, src_ap):
        K = src_ap.shape[0] // 128
        sv = src_ap.rearrange("(k p) n -> p k n", p=128)
        for k in range(K):
            self.p.dma(dst_view[:, k, :], sv[:, k, :], eng="pool", max_dma_last_dim=8192)

    def load_mod(self, mod_d, g_row, j, gate_scale):
        p = self.p
        p.dma(self.SH, mod_d[3 * j:3 * j + 1, :].partition_broadcast(128))
        p.dma(self.Gp, mod_d[3 * j + 1:3 * j + 2, :].partition_broadcast(128))
        p.dma(self.GA, mod_d[3 * j + 2:3 * j + 3, :].partition_broadcast(128))
        gt = self.h[0]
        p.dma(gt, g_row.partition_broadcast(128))
        p.stt(self.Gp, self.Gp, 1.0, gt, ALU.add, ALU.mult)
        if gate_scale != 1.0:
            p.ts(self.GA, self.GA, float(gate_scale), None, op0=ALU.mult, eng="pool")

    def norm_mod(self, xt, hi):
        p = self.p
        ss, rs, h = self.ss[hi], self.rs[hi], self.h[hi]
        p.act(self.junk, xt, AF.Square, accum_out=ss)
        p.act(ss, ss, AF.Sqrt, scale=1.0 / D, bias=self.eps)
        rsa, ssa = rs.t[:], ss.t[:]
        p.op("dve", lambda e: e.reciprocal(rsa, ssa), [ss], [rs])
        p.stt(h, xt, rs[:, 0:1], self.Gp, ALU.mult, ALU.mult)
        p.tt(h, h, self.SH, ALU.add, eng="pool")
        return h

    def to_hT(self, hs, nt):
        p = self.p
        ntile = len(hs)
        for kk in range(4):
            pt = self.pT[kk % 2]
            for kl in range(2):
                k = kk * 2 + kl
                for t in range(ntile):
                    p.transpose(pt[:, kl * GT + t * 128: kl * GT + (t + 1) * 128], hs[t][:, k * 128:(k + 1) * 128], self.ident)
            if ntile == 2:
                p.copy(self.hT4[kk].v.rearrange("p a b -> p (a b)"), pt, eng="act")
            else:
                for kl in range(2):
                    p.copy(self.hT4[kk][:, kl, 0:128], pt[:, kl * GT: kl * GT + 128], eng="act")

    def hTk(self, k, nt):
        return self.hT4[k // 2][:, k % 2, 0:nt]


def token_groups(NL, NC):
    gs = []
    for g in range(NL // GT):
        gs.append((g * GT, 2, False))
    r = NL
    while r < NL + NC:
        n = min(2, (NL + NC - r) // 128)
        gs.append((r, n, True))
        r += n * 128
    return gs


def run_groups(T, x_src, x_dst, NL, NC, modl, modc, g_row, j, gate_scale, body):
    p = T.p
    cur_ctx = None
    for gi, (r0, ntile, is_ctx) in enumerate(token_groups(NL, NC)):
        if cur_ctx != is_ctx:
            T.load_mod(modc if is_ctx else modl, g_row, j, gate_scale)
            cur_ctx = is_ctx
        xts = [T.xt[(gi % 2) * 2 + t] for t in range(ntile)]
        for t in range(ntile):
            p.dma(xts[t], x_src[r0 + t * 128: r0 + (t + 1) * 128, :])
        body(r0, ntile, is_ctx, xts)
        if x_dst is not None:
            for t in range(ntile):
                p.dma(x_dst[r0 + t * 128: r0 + (t + 1) * 128, :], xts[t], eng="act")


def resid_add(T, xt, po, col0, ncol, ti):
    p = T.p
    tmp = T.tmp[ti % 2]
    p.tt(tmp[:, 0:ncol], po[:, 0:ncol], T.GA[:, col0:col0 + ncol], ALU.mult)
    p.tt(xt[:, col0:col0 + ncol], xt[:, col0:col0 + ncol], tmp[:, 0:ncol], ALU.add, eng="pool")


def ffn_pass(T, x_src, x_dst, NL, NC, wi_d, wo_d, modl, modc, g_row, j):
    p = T.p
    T.begin_pass(wA_cols=5632, wB_k=22)
    T.aT = [T.alloc(f"aT{i}", [128, GT], BF16) for i in range(22)]
    T.sg = [T.alloc(f"sg{i}", [128, GT]) for i in range(2)]
    T.load_w(T.wA, wi_d)
    T.load_w(T.wB, wo_d)
    wi, wo = T.wA, T.wB
    st = {"c": 0}

    def body(r0, ntile, is_ctx, xts):
        nt = ntile * 128
        hs = [T.norm_mod(xts[t], t) for t in range(ntile)]
        T.to_hT(hs, nt)
        for jj in range(22):
            c = st["c"]
            st["c"] += 1
            pG, pU = T.pA[(c % 2) * 2], T.pA[(c % 2) * 2 + 1]
            for k in range(8):
                p.mm(pG[:, 0:nt], wi[:, k, jj * 128:(jj + 1) * 128], T.hTk(k, nt), start=(k == 0), stop=(k == 7))
            for k in range(8):
                p.mm(pU[:, 0:nt], wi[:, k, DFF + jj * 128:DFF + (jj + 1) * 128], T.hTk(k, nt), start=(k == 0), stop=(k == 7))
            sg = T.sg[c % 2]
            p.act(sg[:, 0:nt], pG[:, 0:nt], AF.Silu)
            p.tt(T.aT[jj][:, 0:nt], sg[:, 0:nt], pU[:, 0:nt], ALU.mult)
        for t in range(ntile):
            for nh in range(2):
                c = st["c"]
                st["c"] += 1
                po = T.pO[c % 2]
                for jj in range(22):
                    p.mm(po, T.aT[jj][:, t * 128:(t + 1) * 128], wo[:, jj, nh * 512:(nh + 1) * 512],
                         start=(jj == 0), stop=(jj == 21))
                resid_add(T, xts[t], po, nh * 512, 512, c)

    run_groups(T, x_src, x_dst, NL, NC, modl, modc, g_row, j, 0.5, body)

GELU_C = 1.5957691216057308


def s5pre_pass(T, x_src, u_dst, NL, NC, modl, modc, g_row):
    p = T.p
    T.begin_pass()

    def body(r0, ntile, is_ctx, xts):
        for t in range(ntile):
            h = T.norm_mod(xts[t], t)
            p.dma(u_dst[r0 + t * 128: r0 + (t + 1) * 128, :], h, eng="act")

    run_groups(T, x_src, None, NL, NC, modl, modc, g_row, 1, 1.0, body)


def head_rms(T, src, nh, dst, gains):
    p = T.p
    w = nh * 128
    sq = T.hs_sq
    p.act(sq[:, 0:w], src[:, 0:w], AF.Square)
    st = T.hs_st
    p.reduce(st[:, 0:nh], sq[:, 0:w].rearrange("p (h d) -> p h d", d=128), ALU.add)
    p.act(st[:, 0:nh], st[:, 0:nh], AF.Sqrt, scale=1.0 / 128, bias=T.eps)
    sta = st.t[:, 0:nh]
    p.op("dve", lambda e: e.reciprocal(sta, sta), [st], [st])
    bc = View(st, st.t[:, 0:nh].unsqueeze(2).broadcast_to([128, nh, 128]))
    p.tt(dst[:, 0:w].rearrange("p (h d) -> p h d", d=128), src[:, 0:w].rearrange("p (h d) -> p h d", d=128), bc, ALU.mult)
    p.tt(dst[:, 0:w], dst[:, 0:w], gains[:, 0:w], ALU.mult, eng="pool")


def alloc_headstuff(T):
    T.hs_sq = T.alloc("hs_sq", [128, 1280])
    T.hs_st = T.alloc("hs_st", [128, 16])
    T.gains = T.alloc("gains", [128, 1280])


def attpre_pass(T, x_src, qkv_dst, NL, NC, modl, modc, g_row, wqkv_d, qg_d, kg_d, ropeC_d, ropeS_d):
    p = T.p
    T.begin_pass(wA_cols=1536)
    alloc_headstuff(T)
    T.load_w(T.wA, wqkv_d)
    w = T.wA
    for hh in range(8):
        p.dma(T.gains[:, hh * 128:(hh + 1) * 128], qg_d.partition_broadcast(128))
    for hh in range(2):
        p.dma(T.gains[:, 1024 + hh * 128:1024 + (hh + 1) * 128], kg_d.partition_broadcast(128))
    qkv = T.alloc("qkv_sb", [128, 1536])
    qn = T.alloc("qn_sb", [128, 1280])
    ra = T.alloc("ropeA", [128, 1280])
    rb = T.alloc("ropeB", [128, 1280])
    ct = T.alloc("ropeC", [128, 128])
    sn = T.alloc("ropeS", [128, 128])
    ob = T.alloc("qkv_o", [128, 1536], BF16)
    st = {"c": 0}

    def body(r0, ntile, is_ctx, xts):
        nt = ntile * 128
        hs = [T.norm_mod(xts[t], t) for t in range(ntile)]
        T.to_hT(hs, nt)
        for t in range(ntile):
            for cb in range(3):
                c = st["c"]
                st["c"] += 1
                po = T.pA[c % 4]
                for k in range(8):
                    p.mm(po, T.hT4[k // 2][:, k % 2, t * 128:(t + 1) * 128], w[:, k, cb * 512:(cb + 1) * 512],
                         start=(k == 0), stop=(k == 7))
                p.copy(qkv[:, cb * 512:(cb + 1) * 512], po, eng="act")
            head_rms(T, qkv, 10, qn, T.gains)
            if is_ctx:
                p.copy(ob[:, 0:1280], qn, eng="pool")
            else:
                row = r0 + t * 128
                p.dma(ct, ropeC_d[row:row + 128, :])
                p.dma(sn, ropeS_d[row:row + 128, :])
                q3 = qn.v.rearrange("p (h d) -> p h d", d=128)
                cb_ = View(ct, ct.t[:].unsqueeze(1).broadcast_to([128, 10, 128]))
                p.tt(ra.v.rearrange("p (h d) -> p h d", d=128), q3, cb_, ALU.mult)
                q5 = qn.v.rearrange("p (h a f j) -> p h a f j", a=2, f=2, j=32)
                b5 = rb.v.rearrange("p (h a f j) -> p h a f j", a=2, f=2, j=32)
                s4 = sn.t[:].rearrange("p (a f j) -> p a f j", a=2, f=2)
                for f in range(2):
                    sb_ = View(sn, s4[:, :, f, :].unsqueeze(1).broadcast_to([128, 10, 2, 32]))
                    p.tt(b5[:, :, :, f, :], q5[:, :, :, 1 - f, :], sb_, ALU.mult, eng="pool")
                p.tt(ob[:, 0:1280], ra, rb, ALU.add)
            p.copy(ob[:, 1280:1536], qkv[:, 1280:1536], eng="act")
            p.dma(qkv_dst[r0 + t * 128: r0 + (t + 1) * 128, :], ob, eng="act")

    run_groups(T, x_src, None, NL, NC, modl, modc, g_row, 1, 1.0, body)


def proj_resid_pass(T, x_src, x_dst, NL, NC, modl, modc, g_row, aT_d, w_d):
    p = T.p
    T.begin_pass(wB_k=8)
    T.load_w(T.wB, w_d)
    w = T.wB
    av = aT_d.rearrange("(k p) n -> p k n", p=128)
    st = {"c": 0}

    def body(r0, ntile, is_ctx, xts):
        nt = ntile * 128
        for kk in range(4):
            p.dma(T.hT4[kk][:, :, 0:nt], av[:, kk * 2:kk * 2 + 2, r0:r0 + nt])
        for t in range(ntile):
            for nh in range(2):
                c = st["c"]
                st["c"] += 1
                po = T.pO[c % 2]
                for k in range(8):
                    p.mm(po, T.hT4[k // 2][:, k % 2, t * 128:(t + 1) * 128], w[:, k, nh * 512:(nh + 1) * 512],
                         start=(k == 0), stop=(k == 7))
                resid_add(T, xts[t], po, nh * 512, 512, c)

    run_groups(T, x_src, x_dst, NL, NC, modl, modc, g_row, 1, 1.0, body)


def s5post_pass(T, x_src, x_dst, NL, NC, modl, modc, g_row, yT_d, wglu_d):
    p = T.p
    T.begin_pass(wA_cols=2048)
    T.load_w(T.wA, wglu_d)
    w = T.wA
    yv = yT_d.rearrange("(k p) n -> p k n", p=128)
    yin = [T.alloc(f"yin{i}", [128, 2, GT]) for i in range(2)]
    t1 = [T.alloc(f"gl_t{i}", [128, 2, GT]) for i in range(2)]
    sig = T.alloc("glu_sig", [128, 512])
    st = {"c": 0}

    def body(r0, ntile, is_ctx, xts):
        nt = ntile * 128
        for kk in range(4):
            yi, tt_ = yin[kk % 2], t1[kk % 2]
            p.dma(yi[:, :, 0:nt], yv[:, kk * 2:kk * 2 + 2, r0:r0 + nt])
            a, b = yi[:, :, 0:nt], tt_[:, :, 0:nt]
            p.tt(b, a, a, ALU.mult, eng="pool")
            p.ts(b, b, 0.044715, 1.0, op0=ALU.mult, op1=ALU.add)
            p.tt(b, b, a, ALU.mult, eng="pool")
            p.act(b, b, AF.Sigmoid, scale=GELU_C)
            p.tt(T.hT4[kk][:, :, 0:nt], a, b, ALU.mult)
        for t in range(ntile):
            for nh in range(2):
                c = st["c"]
                st["c"] += 1
                pv, pg = T.pA[(c % 2) * 2], T.pA[(c % 2) * 2 + 1]
                for k in range(8):
                    p.mm(pv, T.hT4[k // 2][:, k % 2, t * 128:(t + 1) * 128], w[:, k, nh * 512:(nh + 1) * 512],
                         start=(k == 0), stop=(k == 7))
                for k in range(8):
                    p.mm(pg, T.hT4[k // 2][:, k % 2, t * 128:(t + 1) * 128], w[:, k, 1024 + nh * 512:1024 + (nh + 1) * 512],
                         start=(k == 0), stop=(k == 7))
                p.act(sig, pg, AF.Sigmoid)
                tmp = T.tmp[c % 2]
                p.tt(tmp, pv, sig, ALU.mult)
                p.tt(tmp, tmp, T.GA[:, nh * 512:(nh + 1) * 512], ALU.mult, eng="pool")
                p.tt(xts[t][:, nh * 512:(nh + 1) * 512], xts[t][:, nh * 512:(nh + 1) * 512], tmp, ALU.add)

    run_groups(T, x_src, x_dst, NL, NC, modl, modc, g_row, 1, 1.0, body)


def gdnpre_pass(T, x_src, proj_dst, NL, NC, modl, modc, g_row, win_d):
    p = T.p
    T.begin_pass(wA_cols=4128)
    T.load_w(T.wA, win_d)
    w = T.wA
    pr = [T.alloc(f"proj_sb{i}", [128, 4128]) for i in range(1)]
    st = {"c": 0}

    def body(r0, ntile, is_ctx, xts):
        nt = ntile * 128
        hs = [T.norm_mod(xts[t], t) for t in range(ntile)]
        T.to_hT(hs, nt)
        for t in range(ntile):
            prt = pr[0]
            for cb in range(9):
                ncol = 512 if cb < 8 else 32
                c = st["c"]
                st["c"] += 1
                po = T.pA[c % 4]
                for k in range(8):
                    p.mm(po[:, 0:ncol], T.hT4[k // 2][:, k % 2, t * 128:(t + 1) * 128], w[:, k, cb * 512:cb * 512 + ncol],
                         start=(k == 0), stop=(k == 7))
                p.copy(prt[:, cb * 512:cb * 512 + ncol], po[:, 0:ncol], eng=("act" if cb % 2 == 0 else "dve"))
            p.dma(proj_dst[r0 + t * 128: r0 + (t + 1) * 128, :], prt, eng="act")

    run_groups(T, x_src, None, NL, NC, modl, modc, g_row, 1, 1.0, body)


def gdnpost_pass(T, x_src, x_dst, NL, NC, modl, modc, g_row, o_d, z_d, og_d, wo_d):
    p = T.p
    T.begin_pass(wB_k=8)
    alloc_headstuff(T)
    T.load_w(T.wB, wo_d)
    w = T.wB
    for hh in range(8):
        p.dma(T.gains[:, hh * 128:(hh + 1) * 128], og_d.partition_broadcast(128))
    ot = [T.alloc(f"go_t{i}", [128, D]) for i in range(2)]
    zt = [T.alloc(f"gz_t{i}", [128, D]) for i in range(2)]
    st = {"c": 0}

    def body(r0, ntile, is_ctx, xts):
        nt = ntile * 128
        ys = []
        for t in range(ntile):
            row = r0 + t * 128
            p.dma(ot[t], o_d[0, row:row + 128, :])
            p.dma(zt[t], o_d[1, row:row + 128, :])
            p.tt(ot[t], ot[t], zt[t], ALU.add, eng="pool")
            p.dma(zt[t], z_d[row:row + 128, :])
            head_rms(T, ot[t], 8, ot[t], T.gains)
            p.act(zt[t], zt[t], AF.Silu)
            p.tt(ot[t], ot[t], zt[t], ALU.mult)
            ys.append(ot[t])
        T.to_hT(ys, nt)
        for t in range(ntile):
            for nh in range(2):
                c = st["c"]
                st["c"] += 1
                po = T.pO[c % 2]
                for k in range(8):
                    p.mm(po, T.hT4[k // 2][:, k % 2, t * 128:(t + 1) * 128], w[:, k, nh * 512:(nh + 1) * 512],
                         start=(k == 0), stop=(k == 7))
                resid_add(T, xts[t], po, nh * 512, 512, c)

    run_groups(T, x_src, x_dst, NL, NC, modl, modc, g_row, 1, 1.0, body)

def rope_tables(L):
    t = np.arange(L)
    row = (t // 64).astype(np.float32)
    col = (t % 64).astype(np.float32)
    inv = (np.float32(10000.0) ** (-np.arange(0, 64, 2, dtype=np.float32) / np.float32(64))).astype(np.float32)
    ar = row[:, None] * inv[None, :]
    ac = col[:, None] * inv[None, :]
    cr, sr, cc, sc = np.cos(ar), np.sin(ar), np.cos(ac), np.sin(ac)
    C = np.concatenate([cr, cr, cc, cc], 1).astype(np.float32)
    S = np.concatenate([-sr, sr, -sc, sc], 1).astype(np.float32)
    return C, S

def build_att(NLAT, NCTX):
    NQ = NLAT + NCTX
    NK = NCTX + NLAT
    nkt = NK // 128
    nct = NCTX // 128
    nc = bass.Bass("TRN2", target_bir_lowering=False)
    qT_d = nc.dram_tensor("qT", [4, 128, NQ], BF16, kind="ExternalInput").ap()
    kT_d = nc.dram_tensor("kT", [128, NK], BF16, kind="ExternalInput").ap()
    v_d = nc.dram_tensor("v", [NK, 128], BF16, kind="ExternalInput").ap()
    qg_d = nc.dram_tensor("qg", [1, 128], F32, kind="ExternalInput").ap()
    kg_d = nc.dram_tensor("kg", [1, 128], F32, kind="ExternalInput").ap()
    oT_d = nc.dram_tensor("oT", [4, 128, NQ], BF16, kind="ExternalOutput").ap()
    scale = 128.0 ** -0.5
    with ExitStack() as st:
        p = Prog(nc, st)
        qT = [p.sb(f"qTs{g}", [128, NQ], BF16) for g in range(4)]
        kT = p.sb("kTs", [128, NK], BF16)
        V = p.sb("Vs", [128, nkt, 128], BF16)
        ones = p.sb("ones", [128, 128], BF16)
        onesf = p.sb("onesf", [1, 128])
        gq = p.sb("gqs", [1, 128])
        gk = p.sb("gks", [1, 128])
        mq = p.sb("mq", [1, 1])
        mk = p.sb("mk", [1, 1])
        negb = p.sb("negb", [128, 1])
        pts = [p.sb(f"pt{i}", [128, 512], BF16) for i in range(4)]
        rl = [p.sb(f"rl{i}", [128, 512]) for i in range(2)]
        ob = [p.sb(f"ob{i}", [128, 512], BF16) for i in range(2)]
        pS = [p.ps(f"pS{i}", [128, 512]) for i in range(3)]
        pO = [p.ps(f"pO{i}", [128, 512]) for i in range(2)]
        pL = [p.ps(f"pL{i}", [128, 512]) for i in range(2)]
        for g in range(4):
            p.dma(qT[g], qT_d[g])
        p.dma(kT, kT_d)
        p.dma(V, v_d.rearrange("(t p) d -> p t d", p=128))
        p.dma(gq, qg_d)
        p.dma(gk, kg_d)
        p.memset(ones, 1.0)
        p.memset(onesf, 1.0)
        gqa, gka, mqa, mka = gq.t[:], gk.t[:], mq.t[:], mk.t[:]
        p.act(gq, gq, AF.Abs)
        p.act(gk, gk, AF.Abs)
        p.op("dve", lambda e: e.tensor_reduce(mqa, gqa, AX.X, ALU.max), [gq], [mq])
        p.op("dve", lambda e: e.tensor_reduce(mka, gka, AX.X, ALU.max), [gk], [mk])
        p.tt(mq, mq, mk, ALU.mult)
        p.mm(pS[0][:, 0:1], onesf, mq, start=True, stop=True)
        p.ts(negb, pS[0][:, 0:1], -(128.0 ** 0.5), None, op0=ALU.mult)
        cnt = {"s": 0, "o": 0}

        def block(q0, nq, kts):
            for g in range(4):
                o = cnt["o"]
                cnt["o"] += 1
                po, pl = pO[o % 2], pL[o % 2]
                for i, kt in enumerate(kts):
                    s = cnt["s"]
                    cnt["s"] += 1
                    ps_, pt = pS[s % 3], pts[s % 4]
                    p.mm(ps_[:, 0:nq], kT[:, kt * 128:(kt + 1) * 128], qT[g][:, q0:q0 + nq], start=True, stop=True)
                    p.act(pt[:, 0:nq], ps_[:, 0:nq], AF.Exp, bias=negb, scale=scale)
                    p.mm(po[:, 0:nq], V[:, kt, :], pt[:, 0:nq], start=(i == 0), stop=(i == len(kts) - 1))
                    p.mm(pl[:, 0:nq], ones, pt[:, 0:nq], start=(i == 0), stop=(i == len(kts) - 1))
                r, ot = rl[o % 2], ob[o % 2]
                ra, pla = r.t[:, 0:nq], pl.t[:, 0:nq]
                p.op("dve", lambda e, ra=ra, pla=pla: e.reciprocal(ra, pla), [pl], [r])
                p.tt(ot[:, 0:nq], po[:, 0:nq], r[:, 0:nq], ALU.mult)
                p.dma(oT_d[g][:, q0:q0 + nq], ot[:, 0:nq])

        for qb in range(NLAT // 512):
            block(qb * 512, 512, list(range(nkt)))
        block(NLAT, NCTX, list(range(nct)))
        p.emit()
    return nc

import math
TWO_PI = 2.0 * math.pi


class Arena:
    def __init__(self, p, name, size):
        self.p = p
        self.size = size
        self.t = p.sb(name, [128, size])
        self.off = 0
        self.n = 0

    def alloc(self, name, shape, dtype=F32):
        per = 1
        for s_ in shape[1:]:
            per *= s_
        nbytes = per * (2 if dtype == BF16 else 4)
        n4 = (nbytes + 3) // 4
        assert self.off + n4 <= self.size, (name, self.off, n4, self.size)
        ap = self.t.t[0:shape[0], self.off:self.off + n4]
        self.off += n4
        if dtype == BF16:
            ap = ap.bitcast(BF16)[:, 0:per]
        if len(shape) == 3:
            ap = ap.rearrange("p (a b) -> p a b", a=shape[1])
        self.n += 1
        t = Tile(self.p, f"{name}_{self.n}", shape, dtype, "sb")
        t.t = ap
        self.p.tiles.append(t)
        return t


def s5_abar(p, A, lr, li, ldt, shape, tag):
    dt = A.alloc(f"dt{tag}", shape)
    p.act(dt, ldt, AF.Exp)
    zr = A.alloc(f"zr{tag}", shape)
    ph = A.alloc(f"ph{tag}", shape)
    pc = A.alloc(f"pc{tag}", shape)
    m = A.alloc(f"m{tag}", shape)
    p.tt(zr, lr, dt, ALU.mult)
    p.tt(ph, li, dt, ALU.mult)
    rho = A.alloc(f"rho{tag}", shape)
    p.act(rho, zr, AF.Exp)
    for _ in range(5):
        p.ts(m, ph, math.pi, None, op0=ALU.is_gt)
        p.stt(ph, m, -TWO_PI, ph, ALU.mult, ALU.add)
    p.ts(pc, ph, math.pi / 2, None, op0=ALU.add)
    p.ts(m, pc, math.pi, None, op0=ALU.is_gt)
    p.stt(pc, m, -TWO_PI, pc, ALU.mult, ALU.add)
    sn = A.alloc(f"sn{tag}", shape)
    cs = A.alloc(f"cs{tag}", shape)
    p.act(sn, ph, AF.Sin)
    p.act(cs, pc, AF.Sin)
    return rho, cs, sn


def build_s5(NLAT, NCTX, NB=4):
    NTOK = NCTX + NLAT
    TC = 512
    nc = bass.Bass("TRN2", target_bir_lowering=False)
    u_d = nc.dram_tensor("uT", [128, NB, NTOK], F32, kind="ExternalInput").ap()
    lamc_d = nc.dram_tensor("lam_c", [128, 2, 16], F32, kind="ExternalInput").ap()
    ldtc_d = nc.dram_tensor("ldt_c", [128, 16], F32, kind="ExternalInput").ap()
    lamr_d = nc.dram_tensor("lam_r", [128, 2, 1024], F32, kind="ExternalInput").ap()
    ldtr_d = nc.dram_tensor("ldt_r", [128, 1024], F32, kind="ExternalInput").ap()
    bT_d = nc.dram_tensor("bT", [2, 16, 16, 64], F32, kind="ExternalInput").ap()
    cT_d = nc.dram_tensor("cT", [2, 16, 64, 16], F32, kind="ExternalInput").ap()
    d_d = nc.dram_tensor("dskip", [128, 1], F32, kind="ExternalInput").ap()
    y_d = nc.dram_tensor("yT", [128, NB, NTOK], F32, kind="ExternalOutput").ap()
    with ExitStack() as st:
        p = Prog(nc, st)
        ident = make_ident(p)
        A = Arena(p, "s5arena", 49000)
        pA = [p.ps(f"pA{i}", [128, 512]) for i in range(2)]
        pB = [p.ps(f"pB{i}", [128, 512]) for i in range(2)]
        pY = [p.ps(f"pY{i}", [128, 512]) for i in range(2)]
        pR = p.ps("pR", [128, 512])
        COS = A.alloc("COS", [128, 16, 520])
        SIN = A.alloc("SIN", [128, 16, 520])
        LB = A.alloc("LB", [128, 16, 128], BF16)
        LBs = A.alloc("LBs", [128, 16, 128], BF16)
        L1 = A.alloc("L1", [128, 16, 128], BF16)
        L2 = A.alloc("L2", [128, 16, 128], BF16)
        R5 = A.alloc("R5", [128, 16, 128])
        R2 = A.alloc("R2", [128, 16, 128])
        SW = A.alloc("SW", [128, 128])
        dcol = A.alloc("dcol", [128, 1])
        rho_c = A.alloc("rho_cc", [128, 16])
        ST = A.alloc("ST", [128, 16])
        mark = A.off
        lamc = A.alloc("lamc", [128, 2, 16])
        ldtc = A.alloc("ldtc", [128, 16])
        p.dma(lamc, lamc_d)
        p.dma(ldtc, ldtc_d)
        p.dma(dcol, d_d)
        rho_t, cs_c, sn_c = s5_abar(p, A, lamc[:, 0, :], lamc[:, 1, :], ldtc, [128, 16], "c")
        p.copy(rho_c, rho_t)
        p.memset(COS[:, :, 0:1], 1.0)
        p.memset(SIN[:, :, 0:1], 0.0)
        p.copy(COS[:, :, 1:2], View(cs_c, cs_c.t[:].unsqueeze(2)))
        p.copy(SIN[:, :, 1:2], View(sn_c, sn_c.t[:].unsqueeze(2)))
        t1 = A.alloc("tb1", [128, 16, 256])
        t2 = A.alloc("tb2", [128, 16, 256])
        for k in range(9):
            n = 1 << k
            src = slice(1, n + 1)
            dst = slice(n + 1, 2 * n + 1)
            ck = View(COS, COS.t[:, :, n:n + 1].broadcast_to([128, 16, n]))
            sk = View(SIN, SIN.t[:, :, n:n + 1].broadcast_to([128, 16, n]))
            p.tt(t1[:, :, 0:n], COS[:, :, src], ck, ALU.mult)
            p.tt(t2[:, :, 0:n], SIN[:, :, src], sk, ALU.mult, eng="pool")
            p.tt(COS[:, :, dst], t1[:, :, 0:n], t2[:, :, 0:n], ALU.subtract)
            p.tt(t1[:, :, 0:n], SIN[:, :, src], ck, ALU.mult)
            p.tt(t2[:, :, 0:n], COS[:, :, src], sk, ALU.mult, eng="pool")
            p.tt(SIN[:, :, dst], t1[:, :, 0:n], t2[:, :, 0:n], ALU.add)
        p.memset(SW, 0.0, eng="pool")
        swa = SW.t[:]
        p.op("pool", lambda e: e.affine_select(swa, swa, [[-1, 128]], ALU.not_equal, 1.0, base=64, channel_multiplier=1), [SW], [SW])
        p.op("pool", lambda e: e.affine_select(swa, swa, [[-1, 128]], ALU.not_equal, -1.0, base=-64, channel_multiplier=1), [SW], [SW])
        tmpR = A.alloc("tmpR", [128, 128])
        for (Rm, Tn) in ((R5, 512), (R2, NCTX)):
            for gr in range(16):
                p.ts(tmpR, SW, SIN[:, gr, Tn:Tn + 1], None, op0=ALU.mult)
                p.stt(Rm[:, gr, :], ident, COS[:, gr, Tn:Tn + 1], tmpR, ALU.mult, ALU.add)
        p.ts(SIN[64:128, :, :], SIN[64:128, :, :], -1.0, None, op0=ALU.mult)
        p.barrier()
        A.off = mark
        lamr = A.alloc("lamr", [128, 2, 1024])
        ldtr = A.alloc("ldtr", [128, 1024])
        p.dma(lamr, lamr_d)
        p.dma(ldtr, ldtr_d)
        lr, li = lamr[:, 0, :], lamr[:, 1, :]
        rho_r, cs_r, sn_r = s5_abar(p, A, lr, li, ldtr, [128, 1024], "r")
        ar = A.alloc("ar", [128, 1024])
        ai = A.alloc("ai", [128, 1024])
        den = A.alloc("den", [128, 1024])
        tq = A.alloc("tq", [128, 1024])
        fr = A.alloc("fr", [128, 1024])
        fi = A.alloc("fi", [128, 1024])
        p.tt(ar, rho_r, cs_r, ALU.mult)
        p.ts(ar, ar, -1.0, None, op0=ALU.add)
        p.tt(ai, rho_r, sn_r, ALU.mult)
        p.tt(den, lr, lr, ALU.mult)
        p.tt(tq, li, li, ALU.mult)
        p.tt(den, den, tq, ALU.add)
        dena = den.t[:]
        p.op("dve", lambda e: e.reciprocal(dena, dena), [den], [den])
        p.tt(fr, ar, lr, ALU.mult)
        p.tt(tq, ai, li, ALU.mult)
        p.tt(fr, fr, tq, ALU.add)
        p.tt(fr, fr, den, ALU.mult)
        p.tt(fi, ai, lr, ALU.mult)
        p.tt(tq, ar, li, ALU.mult)
        p.tt(fi, fi, tq, ALU.subtract)
        p.tt(fi, fi, den, ALU.mult)
        BTr = A.alloc("BTr", [128, 16, 64])
        BTi = A.alloc("BTi", [128, 16, 64])
        p.memset(BTr, 0.0)
        p.memset(BTi, 0.0, eng="pool")
        for gr in range(16):
            g = gr % 8
            p.dma(BTr[16 * g:16 * g + 16, gr, :], bT_d[0, gr])
            p.dma(BTi[16 * g:16 * g + 16, gr, :], bT_d[1, gr])
        f3r = fr.v.rearrange("p (a b) -> p a b", a=16)
        f3i = fi.v.rearrange("p (a b) -> p a b", a=16)
        q1 = A.alloc("q1", [128, 16, 64])
        q2 = A.alloc("q2", [128, 16, 64])
        p.tt(q1, f3r, BTr, ALU.mult)
        p.tt(q2, f3i, BTi, ALU.mult, eng="pool")
        p.tt(LB[:, :, 0:64], q1, q2, ALU.subtract)
        p.tt(LBs[:, :, 64:128], q1, q2, ALU.subtract)
        p.tt(q1, f3r, BTi, ALU.mult)
        p.tt(q2, f3i, BTr, ALU.mult, eng="pool")
        p.tt(LB[:, :, 64:128], q1, q2, ALU.add)
        p.tt(LBs[:, :, 0:64], q1, q2, ALU.add)
        CR = A.alloc("CR", [128, 16, 16])
        CI = A.alloc("CI", [128, 16, 16])
        for half in range(2):
            p.dma(CR[64 * half:64 * half + 64, :, :], cT_d[0].rearrange("g p h -> p g h"))
            p.dma(CI[64 * half:64 * half + 64, :, :], cT_d[1].rearrange("g p h -> p g h"))
        p.memset(L1, 0.0)
        p.memset(L2, 0.0, eng="pool")
        for gr in range(16):
            g = gr % 8
            cs_ = slice(16 * g, 16 * g + 16)
            p.copy(L1[0:64, gr, cs_], CR[0:64, gr, :])
            p.ts(L1[64:128, gr, cs_], CI[64:128, gr, :], -1.0, None, op0=ALU.mult)
            p.ts(L2[0:64, gr, cs_], CI[0:64, gr, :], -1.0, None, op0=ALU.mult)
            p.copy(L2[64:128, gr, cs_], CR[64:128, gr, :])
        p.barrier()
        A.off = mark
        yacc = A.alloc("yacc", [128, NTOK])
        NW = 3
        ub = [A.alloc(f"ub{i}", [128, 512], BF16) for i in range(NW)]
        uf = [A.alloc(f"uf{i}", [128, 512]) for i in range(NW)]
        m1 = [A.alloc(f"m1_{i}", [128, 512]) for i in range(NW)]
        m2 = [A.alloc(f"m2_{i}", [128, 512]) for i in range(NW)]
        gs = [A.alloc(f"gs{i}", [128, 512]) for i in range(NW)]
        G1 = [A.alloc(f"G1_{i}", [128, 512], BF16) for i in range(NW)]
        G2 = [A.alloc(f"G2_{i}", [128, 512], BF16) for i in range(NW)]
        cnt = {"w": 0, "c": 0}
        for b in range(NB):
            for r in range(2):
                chunks = [(0, NCTX, True)] + [(NCTX + c * TC, TC, False) for c in range(NLAT // TC)]
                if r == 1:
                    chunks = [chunks[0]] + chunks[:0:-1]
                p.memset(ST, 0.0)
                for (t0, Tn, is_ctx) in chunks:
                    ci = cnt["c"]
                    cnt["c"] += 1
                    ubt, uft = ub[ci % NW], uf[ci % NW]
                    p.dma(uft[:, 0:Tn], u_d[:, b, t0:t0 + Tn])
                    p.copy(ubt[:, 0:Tn], uft[:, 0:Tn], eng="act")
                    if r == 0:
                        p.act(yacc[:, t0:t0 + Tn], uft[:, 0:Tn], AF.Copy, scale=dcol)
                    py = pY[ci % 2]
                    fwd = (r == 0)
                    tab = slice(0, Tn) if fwd else slice(Tn - 1, None, -1) if False else None
                    for g in range(8):
                        gr = r * 8 + g
                        w = cnt["w"]
                        cnt["w"] += 1
                        pa, pb = pA[w % 2], pB[w % 2]
                        p.mm(pa[:, 0:Tn], LB[:, gr, :], ubt[:, 0:Tn], start=True, stop=True)
                        p.mm(pb[:, 0:Tn], LBs[:, gr, :], ubt[:, 0:Tn], start=True, stop=True)
                        if fwd:
                            cosv = COS[:, gr, 0:Tn]
                            sinv = SIN[:, gr, 0:Tn]
                        else:
                            cosv = View(COS, COS.t[:, gr, Tn - 1::-1] if Tn - 1 >= 0 else None)
                            sinv = View(SIN, SIN.t[:, gr, Tn - 1::-1])
                        a1, a2, gt = m1[w % NW], m2[w % NW], gs[w % NW]
                        p.tt(a1[:, 0:Tn], pa[:, 0:Tn], cosv, ALU.mult)
                        p.tt(a2[:, 0:Tn], pb[:, 0:Tn], sinv, ALU.mult)
                        p.tt(a1[:, 0:Tn], a1[:, 0:Tn], a2[:, 0:Tn], ALU.add, eng="pool")
                        if fwd:
                            d1, oo = a1.t[:, 0:Tn], gt.t[:, 0:Tn]
                        else:
                            d1, oo = a1.t[:, Tn - 1::-1], gt.t[:, Tn - 1::-1]
                        d0 = rho_c.t[:, gr:gr + 1].broadcast_to([128, Tn])
                        ini = ST.t[:, gr:gr + 1]
                        p.op("dve", lambda e, oo=oo, d0=d0, d1=d1, ini=ini: e.tensor_tensor_scan(oo, d0, d1, ini, ALU.mult, ALU.add),
                             [a1, rho_c, ST], [gt])
                        g1, g2 = G1[w % NW], G2[w % NW]
                        p.tt(g1[:, 0:Tn], gt[:, 0:Tn], cosv, ALU.mult, eng="pool")
                        p.tt(g2[:, 0:Tn], gt[:, 0:Tn], sinv, ALU.mult, eng="pool")
                        p.mm(py[:, 0:Tn], L1[:, gr, :], g1[:, 0:Tn], start=(g == 0), stop=False)
                        p.mm(py[:, 0:Tn], L2[:, gr, :], g2[:, 0:Tn], start=False, stop=(g == 7))
                        last = (Tn - 1) if fwd else 0
                        Rm = R2 if is_ctx else R5
                        p.mm(pR[:, gr:gr + 1], Rm[:, gr, :], gt[:, last:last + 1], start=True, stop=True)
                        p.copy(ST[:, gr:gr + 1], pR[:, gr:gr + 1], eng="act")
                    p.tt(yacc[:, t0:t0 + Tn], yacc[:, t0:t0 + Tn], py[:, 0:Tn], ALU.add)
            p.dma(y_d[:, b, :], yacc, eng="act")
        p.emit()
    return nc


def s5_host_params(inp, slot, cb):
    gs = slice(8 * cb, 8 * cb + 8)
    lam_re = inp["s5_lam_re"][slot][:, gs].reshape(16, 64)
    lam_im = inp["s5_lam_im"][slot][:, gs].reshape(16, 64)
    ldt = inp["s5_log_dt"][slot][:, gs].reshape(16)
    lam_c = np.zeros((128, 2, 16), np.float32)
    lam_c[:, 0, :] = np.concatenate([lam_re.T, lam_re.T], 0)
    lam_c[:, 1, :] = np.concatenate([lam_im.T, lam_im.T], 0)
    ldt_c = np.ascontiguousarray(np.broadcast_to(ldt[None, :], (128, 16))).astype(np.float32)
    lam_r = np.zeros((128, 2, 1024), np.float32)
    lam_r[:, 0, :] = lam_re.reshape(1, 1024)
    lam_r[:, 1, :] = lam_im.reshape(1, 1024)
    ldt_r = np.ascontiguousarray(np.broadcast_to(np.repeat(ldt, 64)[None, :], (128, 1024))).astype(np.float32)
    b_re = inp["s5_b_re"][slot][:, gs].reshape(16, 64, 16)
    b_im = inp["s5_b_im"][slot][:, gs].reshape(16, 64, 16)
    bT = np.ascontiguousarray(np.stack([b_re.transpose(0, 2, 1), b_im.transpose(0, 2, 1)], 0)).astype(np.float32)
    c_re = inp["s5_c_re"][slot][:, gs].reshape(16, 16, 64)
    c_im = inp["s5_c_im"][slot][:, gs].reshape(16, 16, 64)
    cT = np.ascontiguousarray(np.stack([c_re.transpose(0, 2, 1), c_im.transpose(0, 2, 1)], 0)).astype(np.float32)
    d = np.ascontiguousarray(inp["s5_d"][slot][128 * cb:128 * cb + 128, None]).astype(np.float32)
    return {"lam_c": lam_c, "ldt_c": ldt_c, "lam_r": lam_r, "ldt_r": ldt_r, "bT": bT, "cT": cT, "dskip": d}

def subtile(p, parent, name, c0, n):
    return parent[:, c0:c0 + n]


def build_gdn(NLAT, NCTX, dbg=9):
    NTOK = NCTX + NLAT
    ntile = NTOK // 128
    nct = NCTX // 128
    nc = bass.Bass("TRN2", target_bir_lowering=False)
    x_d = nc.dram_tensor("qkv", [NTOK, 1536], F32, kind="ExternalInput").ap()
    ab_d = nc.dram_tensor("ab", [NTOK, 16], F32, kind="ExternalInput").ap()
    cw_d = nc.dram_tensor("cw", [3, 1536], F32, kind="ExternalInput").ap()
    al_d = nc.dram_tensor("alog", [1, 8], F32, kind="ExternalInput").ap()
    db_d = nc.dram_tensor("dtb", [1, 8], F32, kind="ExternalInput").ap()
    o_d = nc.dram_tensor("o", [2, NTOK, 512], F32, kind="ExternalOutput").ap()
    with ExitStack() as st:
        p = Prog(nc, st)
        ident = make_ident(p)
        ones = p.sb("ones", [128, 128])
        p.memset(ones, 1.0)
        BLK = p.sb("BLK", [128, 128])
        p.memset(BLK, 0.0)
        p.memset(BLK[0:64, 0:64], 1.0)
        p.memset(BLK[64:128, 64:128], 1.0)
        SELa = p.sb("SELa", [128, 128])
        SELb = p.sb("SELb", [128, 128])
        p.memset(SELa, 0.0)
        p.memset(SELa[0:64, :], 1.0)
        p.memset(SELb, 0.0)
        p.memset(SELb[64:128, :], 1.0)

        def tri(name, op, sgn):
            t = p.sb(name, [128, 128])
            ta, oa = t.t[:], ones.t[:]
            p.op("pool", lambda e: e.affine_select(ta, oa, [[-sgn, 128]], op, 0.0, base=0, channel_multiplier=sgn), [ones], [t])
            p.tt(t, t, BLK, ALU.mult, eng="pool")
            return t

        GE = tri("M_ge", ALU.is_ge, 1)
        LE = tri("M_le", ALU.is_ge, -1)
        GT_ = tri("M_gt", ALU.is_gt, 1)
        LT_ = tri("M_lt", ALU.is_gt, -1)
        TRI = [LE, GE]
        INCL = [GE, LE]
        STRICT = [GT_, LT_]
        CW = [p.sb(f"CW{k}", [128, 1536]) for k in range(3)]
        for k in range(3):
            p.dma(CW[k], cw_d[k:k + 1, :].partition_broadcast(128))
        negA = p.sb("negA", [128, 8])
        dtb = p.sb("dtb_s", [128, 8])
        p.dma(negA, al_d.partition_broadcast(128))
        p.dma(dtb, db_d.partition_broadcast(128))
        p.act(negA, negA, AF.Exp)
        p.ts(negA, negA, -1.0, None, op0=ALU.mult)
        eps6 = p.sb("eps6", [128, 1])
        p.memset(eps6, 1e-6)
        NS = 2
        Xm = [p.sb(f"Xm{i}", [128, 1536]) for i in range(NS)]
        X0 = [p.sb(f"X0{i}", [128, 1536]) for i in range(NS)]
        Xp = [p.sb(f"Xp{i}", [128, 1536]) for i in range(NS)]
        QKV = [p.sb(f"QKV{i}", [128, 1536]) for i in range(NS)]
        SQ = p.sb("SQ", [128, 1024])
        NRM = [p.sb(f"NRM{i}", [128, 8]) for i in range(NS)]
        ABt = [p.sb(f"AB{i}", [128, 16]) for i in range(NS)]
        Gt = [p.sb(f"G{i}", [128, 8]) for i in range(NS)]
        Bt = [p.sb(f"Bt{i}", [128, 8]) for i in range(NS)]
        Ot = [p.sb(f"Ot{i}", [128, 512]) for i in range(NS)]
        gcol = [p.sb(f"gcol{i}", [128, 4]) for i in range(NS)]
        egc = [p.sb(f"egc{i}", [128, 4]) for i in range(NS)]
        bkc = [p.sb(f"bkc{i}", [128, 4]) for i in range(NS)]
        kdc = [p.sb(f"kdc{i}", [128, 4]) for i in range(NS)]
        gea = [p.sb(f"gea{i}", [128, 4]) for i in range(NS)]
        geb = [p.sb(f"geb{i}", [128, 4]) for i in range(NS)]

        def U(name, shape=[128, 128]):
            return [p.sb(f"{name}{i}", shape) for i in range(NS)]

        kT, qT, Gb, Dm, egr = U("kT"), U("qT"), U("Gb"), U("Dm"), U("egr")
        Am, N0, NT0, N1, NT1 = U("Am"), U("N0_"), U("NT0_"), U("N1_"), U("NT1_")
        Xs = U("Xs", [128, 256])
        wT, qkT, qdT, kd, vn = U("wT"), U("qkT"), U("qdT"), U("kd"), U("vn")
        for i in range(NS):
            p.memset(vn[i], 0.0)
        S = [p.sb(f"S{h}", [128, 128]) for h in range(4)]
        banks = [p.ps(f"bank{i}", [128, 512]) for i in range(8)]
        pTr = [[subtile(p, banks[s], f"pTr{s}_{j}", j * 128, 128) for j in range(4)] for s in range(2)]
        pMi = [[subtile(p, banks[2 + s], f"pMi{s}_{j}", j * 128, 128) for j in range(4)] for s in range(2)]
        pY = [subtile(p, banks[4], "pY0", 0, 256), subtile(p, banks[5], "pY1", 0, 256)]
        pN = [subtile(p, banks[4], "pN0", 256, 128), subtile(p, banks[5], "pN1", 256, 128)]
        pNT = [subtile(p, banks[4], "pNT0", 384, 128), subtile(p, banks[5], "pNT1", 384, 128)]
        pW = subtile(p, banks[6], "pW", 0, 128)
        pO = subtile(p, banks[6], "pO", 128, 128)
        pS = subtile(p, banks[7], "pS", 0, 128)
        pG = subtile(p, banks[7], "pG", 128, 64)

        def phase1(ti, s):
            t0 = ti * 128
            first = (ti == 0) or (ti == nct)
            last = (ti == nct - 1) or (ti == ntile - 1)
            xm, x0, xp, qkv = Xm[s], X0[s], Xp[s], QKV[s]
            p.dma(x0, x_d[t0:t0 + 128, :])
            if first:
                p.memset(xm[0:32, :], 0.0, eng="pool")
                p.dma(xm[1:128, :], x_d[t0:t0 + 127, :])
            else:
                p.dma(xm, x_d[t0 - 1:t0 + 127, :])
            if last:
                p.memset(xp[96:128, :], 0.0, eng="pool")
                p.dma(xp[0:127, :], x_d[t0 + 1:t0 + 128, :])
            else:
                p.dma(xp, x_d[t0 + 1:t0 + 129, :])
            p.dma(ABt[s], ab_d[t0:t0 + 128, :])
            p.tt(xm, xm, CW[0], ALU.mult)
            p.tt(x0, x0, CW[1], ALU.mult, eng="pool")
            p.tt(xp, xp, CW[2], ALU.mult, eng="pool")
            p.tt(xm, xm, x0, ALU.add)
            p.tt(xm, xm, xp, ALU.add)
            p.act(qkv, xm, AF.Silu)
            p.act(SQ, qkv[:, 0:1024], AF.Square)
            nr = NRM[s]
            p.reduce(nr, SQ.v.rearrange("p (h d) -> p h d", d=128), ALU.add)
            p.act(nr, nr, AF.Sqrt, bias=eps6)
            nra = nr.t[:]
            p.op("dve", lambda e: e.reciprocal(nra, nra), [nr], [nr])
            p.ts(nr[:, 0:4], nr[:, 0:4], 128.0 ** -0.5, None, op0=ALU.mult)
            bc = View(nr, nr.t[:].unsqueeze(2).broadcast_to([128, 8, 128]))
            qk3 = qkv[:, 0:1024].rearrange("p (h d) -> p h d", d=128)
            p.tt(qk3, qk3, bc, ALU.mult)
            g, bt = Gt[s], Bt[s]
            p.tt(g, ABt[s][:, 0:8], dtb, ALU.add)
            p.act(g, g, AF.Exp)
            p.act(g, g, AF.Ln, bias=1.0)
            p.tt(g, g, negA, ALU.mult)
            p.act(bt, ABt[s][:, 8:16], AF.Sigmoid)

        def unit(s, us, r, h):
            qkv, g, bt = QKV[s], Gt[s], Bt[s]
            c = r * 4 + h
            q = qkv[:, h * 128:(h + 1) * 128]
            k = qkv[:, 512 + h * 128:512 + (h + 1) * 128]
            v = qkv[:, 1024 + h * 128:1024 + (h + 1) * 128]
            tr, mi = pTr[us], pMi[us]
            gc_, eg_, bk_, kd_ = gcol[s][:, h:h + 1], egc[s][:, h:h + 1], bkc[s][:, h:h + 1], kdc[s][:, h:h + 1]
            if dbg == 16:
                p.mm(tr[0], ones, ident, start=True, stop=True)
                p.mm(tr[1], BLK, ident, start=True, stop=True)
                p.copy(kT[us], tr[0], eng="act")
                p.copy(qT[us], tr[1], eng="act")
                return
            if dbg == 17:
                p.mm(tr[0], k, ident, start=True, stop=True)
                p.mm(tr[1], q, ident, start=True, stop=True)
                p.copy(kT[us], tr[0], eng="dve")
                p.copy(qT[us], tr[1], eng="dve")
                return
            p.mm(tr[0], k, ident, start=True, stop=True)
            p.mm(tr[1], q, ident, start=True, stop=True)
            if dbg == 10:
                return
            p.copy(kT[us], tr[0], eng="act")
            p.copy(qT[us], tr[1], eng="act")
            if dbg == 11:
                return
            p.ts(Gb[us], ones, g[:, c:c + 1], None, op0=ALU.mult, eng="pool")
            p.mm(mi[0], Gb[us], TRI[r], start=True, stop=True)
            p.ts(Dm[us], mi[0], gc_, 0.0, op0=ALU.subtract, op1=ALU.max)
            p.act(Dm[us], Dm[us], AF.Exp, scale=-1.0)
            p.tt(Dm[us], Dm[us], INCL[r], ALU.mult, eng="pool")
            p.act(egr[us], mi[0], AF.Exp)
            if dbg == 12:
                return
            p.mm(mi[1], kT[us], kT[us], start=True, stop=True)
            p.tt(Am[us], mi[1], Dm[us], ALU.mult)
            p.stt(Am[us], Am[us], bt[:, c:c + 1], STRICT[r], ALU.mult, ALU.mult)
            p.ts(N0[us], Am[us], -1.0, None, op0=ALU.mult, eng="pool")
            p.mm(tr[2], Am[us], ident, start=True, stop=True)
            p.ts(NT0[us], tr[2], -1.0, None, op0=ALU.mult)
            if dbg == 13:
                return
            p.mm(mi[2], qT[us], kT[us], start=True, stop=True)
            p.tt(Am[us], mi[2], Dm[us], ALU.mult)
            p.mm(tr[3], Am[us], ident, start=True, stop=True)
            p.copy(qkT[us], tr[3], eng="act")
            p.tt(qdT[us], qT[us], egr[us], ALU.mult, eng="pool")
            p.ts(kd[us], k, kd_, None, op0=ALU.mult, eng="pool")
            if dbg == 14:
                return
            X = Xs[us]
            p.ts(X[:, 0:128], v, bt[:, c:c + 1], None, op0=ALU.mult)
            p.ts(X[:, 128:256], k, bk_, None, op0=ALU.mult)
            Ncur, NTcur = N0[us], NT0[us]
            Nnxt, NTnxt = N1[us], NT1[us]
            for lv in range(6):
                py = pY[lv % 2]
                p.mm(py, NTcur, X, start=True, stop=True)
                if lv < 5:
                    p.mm(pN[lv % 2], NTcur, Ncur, start=True, stop=True)
                    p.mm(pNT[lv % 2], Ncur, NTcur, start=True, stop=True)
                p.tt(X, X, py, ALU.add)
                if lv < 5:
                    p.copy(Nnxt, pN[lv % 2], eng="act")
                    p.copy(NTnxt, pNT[lv % 2], eng="act")
                    Ncur, Nnxt = Nnxt, Ncur
                    NTcur, NTnxt = NTnxt, NTcur
            if dbg == 15:
                return
            p.mm(tr[0], X[:, 128:256], ident, start=True, stop=True)
            p.copy(wT[us], tr[0], eng="act")

        def steps(s, us, r, h, o_t):
            X = Xs[us]
            order = [0, 1] if r == 0 else [1, 0]
            for cidx in order:
                rows = slice(64 * cidx, 64 * cidx + 64)
                ge = (gea if cidx == 0 else geb)[s][:, h:h + 1]
                p.mm(pW[rows, :], wT[us][:, rows], S[h], start=True, stop=True)
                p.tt(vn[us][rows, :], X[rows, 0:128], pW[rows, :], ALU.subtract)
                p.mm(pO[rows, :], qdT[us][:, rows], S[h], start=True, stop=False)
                p.mm(pO[rows, :], qkT[us][:, rows], vn[us], start=False, stop=True)
                p.mm(pS, kd[us][rows, :], vn[us][rows, :], start=True, stop=True)
                p.copy(o_t[rows, h * 128:(h + 1) * 128], pO[rows, :], eng="act")
                p.stt(S[h], S[h], ge, pS, ALU.mult, ALU.add)

        ucnt = {"u": 0}
        for r in range(2):
            segs = [list(range(0, nct)), list(range(nct, ntile))]
            if r == 1:
                segs = [s_[::-1] for s_ in segs]
            for h in range(4):
                p.memset(S[h], 0.0)
            seq = segs[0] + segs[1]
            for n_, ti in enumerate(seq):
                s = n_ % NS
                if dbg == 19:
                    p.memset(QKV[s], 1.0)
                    for h in range(4):
                        us = ucnt["u"] % NS
                        ucnt["u"] += 1
                        p.mm(pTr[us][0], ones, ident, start=True, stop=True)
                        p.mm(pTr[us][1], BLK, ident, start=True, stop=True)
                        p.copy(kT[us], pTr[us][0], eng="act")
                        p.copy(qT[us], pTr[us][1], eng="act")
                    p.copy(Ot[s], QKV[s][:, 0:512])
                    p.dma(o_d[r, ti * 128:(ti + 1) * 128, :], Ot[s], eng="act")
                    continue
                phase1(ti, s)
                g = Gt[s]
                if dbg < 1:
                    p.copy(Ot[s], QKV[s][:, 0:512])
                    p.dma(o_d[r, ti * 128:(ti + 1) * 128, :], Ot[s], eng="act")
                    continue
                gsl = g[:, r * 4:r * 4 + 4]
                if dbg == 18:
                    for h in range(4):
                        us = ucnt["u"] % NS
                        ucnt["u"] += 1
                        p.mm(pTr[us][0], ones, ident, start=True, stop=True)
                        p.mm(pTr[us][1], BLK, ident, start=True, stop=True)
                        p.copy(kT[us], pTr[us][0], eng="act")
                        p.copy(qT[us], pTr[us][1], eng="act")
                    p.copy(Ot[s], QKV[s][:, 0:512])
                    p.dma(o_d[r, ti * 128:(ti + 1) * 128, :], Ot[s], eng="act")
                    continue
                p.mm(pG[:, 0:4], TRI[r], gsl, start=True, stop=True)
                p.mm(pG[:, 4:8], BLK, gsl, start=True, stop=True)
                p.mm(pG[:, 8:12], SELa, gsl, start=True, stop=True)
                p.mm(pG[:, 12:16], SELb, gsl, start=True, stop=True)
                p.copy(gcol[s], pG[:, 0:4])
                p.act(egc[s], pG[:, 0:4], AF.Exp)
                p.tt(bkc[s], egc[s], Bt[s][:, r * 4:r * 4 + 4], ALU.mult)
                p.tt(kdc[s], pG[:, 4:8], gcol[s], ALU.subtract)
                p.act(kdc[s], kdc[s], AF.Exp)
                p.act(gea[s], pG[:, 8:12], AF.Exp)
                p.act(geb[s], pG[:, 12:16], AF.Exp)
                for h in range(4):
                    us = ucnt["u"] % NS
                    ucnt["u"] += 1
                    if dbg >= 2:
                        unit(s, us, r, h)
                    if dbg in (3, 9):
                        steps(s, us, r, h, Ot[s])
                if dbg not in (3, 9):
                    p.copy(Ot[s], QKV[s][:, 0:512])
                p.dma(o_d[r, ti * 128:(ti + 1) * 128, :], Ot[s], eng="act")
        p.emit()
    return nc

NLAT_FULL, NCTX_FULL, BATCH = 8192, 256, 4
NL_CORE, NC_CORE = 4096, 128


def build_T(specs, NL=None, NC=NC_CORE):
    NL = NL_CORE if NL is None else NL
    nc = bass.Bass("TRN2", target_bir_lowering=False)
    NT = NL + NC

    def din(name, shape, dt=F32):
        return nc.dram_tensor(name, list(shape), dt, kind="ExternalInput").ap()

    def dout(name, shape, dt=F32):
        return nc.dram_tensor(name, list(shape), dt, kind="ExternalOutput").ap()

    x_in = din("x_in", [NT, D])
    x_out = dout("x_out", [NT, D])
    nlay = 1 + max(s[1] for s in specs)
    modl = [din(f"modl{i}", [9, D]) for i in range(nlay)]
    modc = [din(f"modc{i}", [9, D]) for i in range(nlay)]
    gs = [din(f"g{i}", [3, D]) for i in range(nlay)]
    with ExitStack() as st:
        p = Prog(nc, st)
        T = TCtx(p, nc)
        cur = x_in
        for s in specs:
            kind, lay = s[0], s[1]
            ml, mc, g = modl[lay], modc[lay], gs[lay]
            if kind == "ffn":
                j, tag = s[2], s[3]
                wi = din(f"wi{tag}", [D, 2 * DFF])
                wo = din(f"wo{tag}", [DFF, D])
                ffn_pass(T, cur, x_out, NL, NC, wi, wo, ml, mc, g[(0 if j == 0 else 2):(1 if j == 0 else 3), :], j)
                cur = x_out
            elif kind == "s5pre":
                u = dout("u_out", [NT, D])
                s5pre_pass(T, cur, u, NL, NC, ml, mc, g[1:2, :])
            elif kind == "s5post":
                yT = din("yT", [D, NT])
                wg = din("wglu", [D, 2 * D])
                s5post_pass(T, cur, x_out, NL, NC, ml, mc, g[1:2, :], yT, wg)
                cur = x_out
            elif kind == "attpre":
                wq = din("wqkv", [D, 1536])
                qg = din("qg", [1, 128])
                kg = din("kg", [1, 128])
                rc = din("ropeC", [NL, 128])
                rs_ = din("ropeS", [NL, 128])
                qkv = dout("qkv_out", [NT, 1536], BF16)
                attpre_pass(T, cur, qkv, NL, NC, ml, mc, g[1:2, :], wq, qg, kg, rc, rs_)
            elif kind == "attpost":
                aT = din("aT", [D, NT], BF16)
                w = din("w_o", [D, D])
                proj_resid_pass(T, cur, x_out, NL, NC, ml, mc, g[1:2, :], aT, w)
                cur = x_out
            elif kind == "gdnpre":
                w = din("w_in", [D, 4128])
                pr = dout("proj_out", [NT, 4128])
                gdnpre_pass(T, cur, pr, NL, NC, ml, mc, g[1:2, :], w)
            elif kind == "gdnpost":
                o2 = din("o2", [2, NT, D])
                z = din("z", [NT, D])
                og = din("og", [1, 128])
                w = din("gw_o", [D, D])
                gdnpost_pass(T, cur, x_out, NL, NC, ml, mc, g[1:2, :], o2, z, og, w)
                cur = x_out
            else:
                raise ValueError(kind)
        p.emit()
    return nc


def to_cores(al, ac):
    outs = []
    for c in range(8):
        b, hf = c // 2, c % 2
        outs.append(np.ascontiguousarray(np.concatenate(
            [al[b, hf * NL_CORE:(hf + 1) * NL_CORE], ac[b, hf * NC_CORE:(hf + 1) * NC_CORE]], 0)))
    return outs


def from_cores(arrs):
    F_ = arrs[0].shape[-1]
    al = np.zeros((BATCH, NLAT_FULL, F_), arrs[0].dtype)
    ac = np.zeros((BATCH, NCTX_FULL, F_), arrs[0].dtype)
    for c in range(8):
        b, hf = c // 2, c % 2
        al[b, hf * NL_CORE:(hf + 1) * NL_CORE] = arrs[c][:NL_CORE]
        ac[b, hf * NC_CORE:(hf + 1) * NC_CORE] = arrs[c][NL_CORE:]
    return al, ac


def featmajor_to_cores(full):
    outs = []
    for c in range(8):
        b, hf = c // 2, c % 2
        f = full[b]
        outs.append(np.ascontiguousarray(np.concatenate(
            [f[:, hf * NL_CORE:(hf + 1) * NL_CORE], f[:, NLAT_FULL + hf * NC_CORE:NLAT_FULL + (hf + 1) * NC_CORE]], 1)))
    return outs


def run(nc, in_maps):
    res = run_bass_kernel_spmd(nc, in_maps, core_ids=list(range(8)))
    return res.results


def kernel(**inp):
    global NLAT_FULL, NL_CORE
    inp = {k: np.asarray(v) for k, v in inp.items()}
    NLAT_FULL = inp["x"].shape[1]
    NL_CORE = NLAT_FULL // 2
    mod = run_mod(inp)
    ropeC, ropeS = rope_tables(NLAT_FULL)

    def mods(core, layers):
        b = core // 2
        d = {}
        for i, l in enumerate(layers):
            d[f"modl{i}"] = np.ascontiguousarray(mod[l, b])
            d[f"modc{i}"] = np.ascontiguousarray(mod[l, 4])
            d[f"g{i}"] = np.ascontiguousarray(inp["norm_g"][l])
        return d

    def ffw(tag, l, s):
        return {f"wi{tag}": inp["ffn_wi"][l, s], f"wo{tag}": inp["ffn_wo"][l, s]}

    xs = to_cores(inp["x"], inp["ctx"])

    def s5_core(us, slot):
        ul, uc = from_cores(us)
        maps = []
        for cb in range(8):
            sl = slice(128 * cb, 128 * cb + 128)
            uT = np.concatenate([uc[:, :, sl], ul[:, :, sl]], 1).transpose(2, 0, 1)
            m = {"uT": np.ascontiguousarray(uT)}
            m.update(s5_host_params(inp, slot, cb))
            maps.append(m)
        res = run(build_s5(NLAT_FULL, NCTX_FULL, BATCH), maps)
        full = []
        for b in range(BATCH):
            yb = np.concatenate([res[cb]["yT"][:, b, :] for cb in range(8)], 0)
            full.append(np.concatenate([yb[:, NCTX_FULL:], yb[:, :NCTX_FULL]], 1))
        return featmajor_to_cores(full)

    nc = build_T([("ffn", 0, 0, "a"), ("s5pre", 0)])
    maps = [dict(x_in=xs[c], **mods(c, [0]), **ffw("a", 0, 0)) for c in range(8)]
    res = run(nc, maps)
    xs = [r["x_out"] for r in res]
    yTs = s5_core([r["u_out"] for r in res], 0)
    nc = build_T([("s5post", 0), ("ffn", 0, 2, "a"), ("ffn", 1, 0, "b"), ("attpre", 1)])
    maps = []
    for c in range(8):
        hf = c % 2
        m = dict(x_in=xs[c], yT=yTs[c], wglu=inp["s5_w_glu"][0], **mods(c, [0, 1]), **ffw("a", 0, 1), **ffw("b", 1, 0))
        m.update(wqkv=inp["attn_w_qkv"][0], qg=inp["attn_q_gain"][0:1], kg=inp["attn_k_gain"][0:1],
                 ropeC=np.ascontiguousarray(ropeC[hf * NL_CORE:(hf + 1) * NL_CORE]),
                 ropeS=np.ascontiguousarray(ropeS[hf * NL_CORE:(hf + 1) * NL_CORE]))
        maps.append(m)
    res = run(nc, maps)
    xs = [r["x_out"] for r in res]
    ql, qc = from_cores([r["qkv_out"] for r in res])
    maps = []
    for c in range(8):
        b, kvh = c // 2, c % 2
        qT = np.stack([np.concatenate([ql[b][:, (kvh * 4 + g) * 128:(kvh * 4 + g + 1) * 128],
                                       qc[b][:, (kvh * 4 + g) * 128:(kvh * 4 + g + 1) * 128]], 0).T for g in range(4)], 0)
        ks = slice(1024 + kvh * 128, 1024 + (kvh + 1) * 128)
        vs = slice(1280 + kvh * 128, 1280 + (kvh + 1) * 128)
        kT = np.concatenate([qc[b][:, ks], ql[b][:, ks]], 0).T
        v = np.concatenate([qc[b][:, vs], ql[b][:, vs]], 0)
        maps.append({"qT": np.ascontiguousarray(qT), "kT": np.ascontiguousarray(kT), "v": np.ascontiguousarray(v),
                     "qg": inp["attn_q_gain"][0:1], "kg": inp["attn_k_gain"][0:1]})
    res = run(build_att(NLAT_FULL, NCTX_FULL), maps)
    full = []
    for b in range(BATCH):
        full.append(np.concatenate([res[2 * b + kvh]["oT"][g] for kvh in range(2) for g in range(4)], 0))
    aTs = featmajor_to_cores(full)
    nc = build_T([("attpost", 0), ("ffn", 0, 2, "a"), ("ffn", 1, 0, "b"), ("gdnpre", 1)])
    maps = [dict(x_in=xs[c], aT=aTs[c], w_o=inp["attn_w_o"][0], w_in=inp["gdn_w_in"][0], **mods(c, [1, 2]),
                 **ffw("a", 1, 1), **ffw("b", 2, 0)) for c in range(8)]
    res = run(nc, maps)
    xs = [r["x_out"] for r in res]
    projs = [r["proj_out"] for r in res]
    pl, pc = from_cores(projs)
    def sel3(a, hh):
        hs = slice(512 * hh, 512 * hh + 512)
        return np.concatenate([a[..., 0:1024][..., hs], a[..., 1024:2048][..., hs], a[..., 2048:3072][..., hs]], -1)

    maps = []
    for c in range(8):
        b, hh = c // 2, c % 2
        P = np.concatenate([pc[b], pl[b]], 0)
        ab = P[:, 4096:].reshape(-1, 2, 2, 8)[:, :, :, 4 * hh:4 * hh + 4].reshape(-1, 16)
        maps.append({"qkv": np.ascontiguousarray(sel3(P, hh)), "ab": np.ascontiguousarray(ab),
                     "cw": np.ascontiguousarray(sel3(inp["gdn_conv_w"][0], hh)),
                     "alog": np.ascontiguousarray(inp["gdn_a_log"][0][:, 4 * hh:4 * hh + 4].reshape(1, 8)),
                     "dtb": np.ascontiguousarray(inp["gdn_dt_bias"][0][:, 4 * hh:4 * hh + 4].reshape(1, 8))})
    res = run(build_gdn(NLAT_FULL, NCTX_FULL), maps)
    o2s = []
    for c in range(8):
        b, hf = c // 2, c % 2
        o = np.concatenate([res[2 * b]["o"], res[2 * b + 1]["o"]], -1)
        lat = o[:, NCTX_FULL + hf * NL_CORE:NCTX_FULL + (hf + 1) * NL_CORE]
        cx = o[:, hf * NC_CORE:(hf + 1) * NC_CORE]
        o2s.append(np.ascontiguousarray(np.concatenate([lat, cx], 1)))
    nc = build_T([("gdnpost", 0), ("ffn", 0, 2, "a"), ("ffn", 1, 0, "b"), ("s5pre", 1)])
    maps = [dict(x_in=xs[c], o2=o2s[c], z=np.ascontiguousarray(projs[c][:, 3072:4096]), og=inp["gdn_o_gain"][0:1],
                 gw_o=inp["gdn_w_o"][0], **mods(c, [2, 3]), **ffw("a", 2, 1), **ffw("b", 3, 0)) for c in range(8)]
    res = run(nc, maps)
    xs = [r["x_out"] for r in res]
    yTs = s5_core([r["u_out"] for r in res], 1)
    nc = build_T([("s5post", 0), ("ffn", 0, 2, "a")])
    maps = [dict(x_in=xs[c], yT=yTs[c], wglu=inp["s5_w_glu"][1], **mods(c, [3]), **ffw("a", 3, 1)) for c in range(8)]
    res = run(nc, maps)
    xl, _ = from_cores([r["x_out"] for r in res])
    return xl.astype(np.float32)
```

```python
import numpy as np
from contextlib import ExitStack
import concourse.bass as bass
import concourse.mybir as mybir
from concourse.bass_utils import run_bass_kernel_spmd

F32 = mybir.dt.float32
BF16 = mybir.dt.bfloat16
ALU = mybir.AluOpType
AF = mybir.ActivationFunctionType
AX = mybir.AxisListType

EPOCH = 16000


class Tile:
    def __init__(self, prog, name, shape, dtype, space):
        self.prog = prog
        self.name = name
        self.shape = shape
        self.dtype = dtype
        self.space = space
        self.t = None
        self.last_w = None
        self.readers = {}
        self.sem = None
        self.dma_count = 0

    def __getitem__(self, idx):
        return View(self, self.t[idx])

    @property
    def v(self):
        return View(self, self.t[:])


class View:
    def __init__(self, tile, ap):
        self.tile = tile
        self.ap = ap

    def __getitem__(self, idx):
        return View(self.tile, self.ap[idx])

    def rearrange(self, *a, **k):
        return View(self.tile, self.ap.rearrange(*a, **k))

    def bitcast(self, dt):
        return View(self.tile, self.ap.bitcast(dt))


def _ap(x):
    if isinstance(x, View):
        return x.ap
    if isinstance(x, Tile):
        return x.t[:]
    return x


def _tile(x):
    if isinstance(x, View):
        return x.tile
    if isinstance(x, Tile):
        return x
    return None


class Prog:
    ENGS = ("pe", "dve", "act", "pool", "sp")

    def __init__(self, nc, stack):
        self.nc = nc
        self.stack = stack
        self.ops = {e: [] for e in self.ENGS}
        self.count = {e: 0 for e in self.ENGS}
        self.esems = {e: [] for e in self.ENGS}
        self.known = {e: {} for e in self.ENGS}
        self.tiles = []
        self.nsem = 0
        self.out_dma_tiles = []
        self.pending_waits = {e: [] for e in self.ENGS}

    def sb(self, name, shape, dtype=F32):
        t = Tile(self, name, shape, dtype, "sb")
        t.t = self.stack.enter_context(self.nc.sbuf_tensor(name, list(shape), dtype))
        self.tiles.append(t)
        return t

    def ps(self, name, shape, dtype=F32):
        t = Tile(self, name, shape, dtype, "ps")
        t.t = self.stack.enter_context(self.nc.psum_tensor(name, list(shape), dtype))
        self.tiles.append(t)
        return t

    def _newsem(self, name):
        self.nsem += 1
        return self.stack.enter_context(self.nc.semaphore(name))

    def _esem(self, eng, epoch):
        while len(self.esems[eng]) <= epoch:
            self.esems[eng].append(self._newsem(f"s_{eng}_{len(self.esems[eng])}"))
        return self.esems[eng][epoch]

    def _deps(self, eng, reads, writes):
        deps = []
        for x in reads:
            t = _tile(x)
            if t is None:
                continue
            if t.last_w is not None:
                deps.append(t.last_w)
            if t.space == "ps":
                for ev in t.readers.values():
                    if not (ev[0] == "eng" and ev[1] == eng):
                        deps.append(ev)
        for x in writes:
            t = _tile(x)
            if t is None:
                continue
            if t.last_w is not None:
                deps.append(t.last_w)
            deps.extend(t.readers.values())
        waits = []
        kn = self.known[eng]
        for d in deps:
            if d[0] == "eng":
                _, e, idx = d
                if e == eng and eng == "pe":
                    continue
                epoch, k = divmod(idx, EPOCH)
                sem = self._esem(e, epoch)
                val = k + 1
            else:
                _, t, cnt = d
                sem = t.sem
                val = 16 * cnt
            key = id(sem)
            if kn.get(key, 0) >= val:
                continue
            kn[key] = val
            waits.append((sem, val))
        best = {}
        for s, v in waits:
            if id(s) not in best or best[id(s)][1] < v:
                best[id(s)] = (s, v)
        return list(best.values())

    def _record(self, ev, reads, writes):
        for x in writes:
            t = _tile(x)
            if t is None:
                continue
            t.last_w = ev
            t.readers = {}
        for x in reads:
            t = _tile(x)
            if t is None:
                continue
            key = (ev[0], ev[1]) if ev[0] == "eng" else ("dma", id(ev[1]))
            t.readers[key] = ev

    def op(self, eng, fn, reads, writes):
        waits = self._deps(eng, reads, writes)
        idx = self.count[eng]
        self.count[eng] += 1
        epoch, k = divmod(idx, EPOCH)
        sem = self._esem(eng, epoch)
        waits = self.pending_waits[eng] + waits
        self.pending_waits[eng] = []
        self.ops[eng].append((waits, fn, sem, 1))
        self._record(("eng", eng, idx), reads, writes)

    def dma(self, out, in_, eng="sp", **kw):
        to, ti = _tile(out), _tile(in_)
        owner = to if to is not None else ti
        assert owner is not None
        if owner.sem is None:
            owner.sem = self._newsem(f"d_{owner.name}")
        reads = [in_] if ti is not None else []
        writes = [out] if to is not None else []
        waits = self._deps(eng, reads, writes)
        owner.dma_count += 1
        o, i = _ap(out), _ap(in_)
        waits = self.pending_waits[eng] + waits
        self.pending_waits[eng] = []
        self.ops[eng].append((waits, lambda e: e.dma_start(out=o, in_=i, **kw), owner.sem, 16))
        self._record(("dma", owner, owner.dma_count), reads, writes)
        if to is None and owner not in self.out_dma_tiles:
            self.out_dma_tiles.append(owner)

    def barrier(self):
        for eng in self.ENGS:
            kn = self.known[eng]
            waits = []
            for e in self.ENGS:
                n = self.count[e]
                if n == 0 or e == "sp":
                    continue
                epoch, k = divmod(n - 1, EPOCH)
                sem = self._esem(e, epoch)
                if kn.get(id(sem), 0) < k + 1:
                    kn[id(sem)] = k + 1
                    waits.append((sem, k + 1))
            for t in self.tiles:
                if t.sem is not None and t.dma_count > 0:
                    v = 16 * t.dma_count
                    if kn.get(id(t.sem), 0) < v:
                        kn[id(t.sem)] = v
                        waits.append((t.sem, v))
            if waits:
                self.pending_waits[eng].extend(waits)

    def mm(self, out, lhsT, rhs, start=True, stop=True, **kw):
        o, l, r = _ap(out), _ap(lhsT), _ap(rhs)
        self.op("pe", lambda e: e.matmul(o, l, r, start=start, stop=stop, **kw), [lhsT, rhs], [out])

    def transpose(self, out, in_, ident):
        o, i, d = _ap(out), _ap(in_), _ap(ident)
        self.op("pe", lambda e: e.transpose(o, i, d), [in_, ident], [out])

    def act(self, out, in_, func, bias=None, scale=None, accum_out=None, eng="act"):
        o, i = _ap(out), _ap(in_)
        kw = {}
        reads = [in_]
        writes = [out]
        if bias is not None:
            kw["bias"] = _ap(bias)
            reads.append(bias)
        if scale is not None:
            kw["scale"] = _ap(scale)
            reads.append(scale)
        if accum_out is not None:
            kw["accum_out"] = _ap(accum_out)
            writes.append(accum_out)
        self.op("act", lambda e: e.activation(o, i, func, **kw), reads, writes)

    def tt(self, out, in0, in1, op, eng="dve"):
        o, a, b = _ap(out), _ap(in0), _ap(in1)
        self.op(eng, lambda e: e.tensor_tensor(o, a, b, op), [in0, in1], [out])

    def ts(self, out, in0, s1, s2=None, op0=ALU.mult, op1=None, eng="dve", accum_out=None):
        o, a = _ap(out), _ap(in0)
        reads = [in0]
        writes = [out]
        s1a, s2a = _ap(s1), _ap(s2)
        if _tile(s1) is not None:
            reads.append(s1)
        if _tile(s2) is not None:
            reads.append(s2)
        kw = {}
        if op1 is not None:
            kw["op1"] = op1
        if accum_out is not None:
            kw["accum_out"] = _ap(accum_out)
            writes.append(accum_out)
        self.op(eng, lambda e: e.tensor_scalar(o, a, s1a, s2a, op0, **kw), reads, writes)

    def stt(self, out, in0, scalar, in1, op0, op1, eng="dve"):
        o, a, b = _ap(out), _ap(in0), _ap(in1)
        s = _ap(scalar)
        reads = [in0, in1]
        if _tile(scalar) is not None:
            reads.append(scalar)
        self.op(eng, lambda e: e.scalar_tensor_tensor(o, a, s, b, op0, op1), reads, [out])

    def copy(self, out, in_, eng="dve"):
        o, i = _ap(out), _ap(in_)
        if eng == "act":
            self.op("act", lambda e: e.activation(o, i, AF.Copy), [in_], [out])
        else:
            self.op(eng, lambda e: e.tensor_copy(o, i), [in_], [out])

    def memset(self, out, val, eng="dve"):
        o = _ap(out)
        self.op(eng, lambda e: e.memset(o, val), [], [out])

    def reduce(self, out, in_, op, axis=AX.X, eng="dve"):
        o, i = _ap(out), _ap(in_)
        self.op(eng, lambda e: e.tensor_reduce(o, i, axis, op), [in_], [out])

    def emit(self):
        nc = self.nc
        final_waits = []
        for t in self.tiles:
            if t.sem is not None and t.dma_count > 0:
                final_waits.append((t.sem, 16 * t.dma_count))
        ops = self.ops
        engmap = {"pe": "tensor", "dve": "vector", "act": "scalar", "pool": "gpsimd", "sp": "sync"}
        with nc.Block() as block:
            for ename, bname in engmap.items():
                lst = ops[ename]
                fw = final_waits if ename == "sp" else []

                def body(eng, lst=lst, fw=fw):
                    for waits, fn, sem, inc in lst:
                        for s, v in waits:
                            eng.wait_ge(s, v)
                        fn(eng).then_inc(sem, inc)
                    for s, v in fw:
                        eng.wait_ge(s, v)

                if lst or fw:
                    getattr(block, bname)(body)

D = 1024
NMOD = 9
DEPTH = 4
DFF = 2816


def make_ident(p, name="ident"):
    ident = p.sb(name, [128, 128])
    p.memset(ident, 0.0, eng="pool")
    ia = ident.t[:]
    p.op("pool", lambda e: e.affine_select(ia, ia, [[-1, 128]], ALU.not_equal, 1.0, base=0, channel_multiplier=1),
         [ident], [ident])
    return ident


def build_mod():
    nc = bass.Bass("TRN2", target_bir_lowering=False)
    cc = nc.dram_tensor("cc", [8, D], F32, kind="ExternalInput").ap()
    w = nc.dram_tensor("w", [D, 4608], F32, kind="ExternalInput").ap()
    b = nc.dram_tensor("b", [1, 4608], F32, kind="ExternalInput").ap()
    out = nc.dram_tensor("mod", [8, 4608], F32, kind="ExternalOutput").ap()
    with ExitStack() as st:
        p = Prog(nc, st)
        ident = make_ident(p)
        c_sb = p.sb("c_sb", [8, D])
        s_sb = p.sb("s_sb", [8, D])
        scT = p.sb("scT", [128, 8, 8])
        pT = p.ps("pT", [128, 512])
        p.dma(c_sb, cc)
        p.act(s_sb, c_sb, AF.Silu)
        for k in range(8):
            p.transpose(pT[:, k * 8:(k + 1) * 8], s_sb[:, k * 128:(k + 1) * 128], ident[0:8, 0:8])
        p.copy(scT.v.rearrange("p k m -> p (k m)"), pT[:, 0:64])
        wts = [p.sb(f"w{i}", [128, 8, 512]) for i in range(2)]
        bts = [p.sb(f"b{i}", [8, 512]) for i in range(2)]
        pos = [p.ps(f"po{i}", [128, 512]) for i in range(2)]
        ots = [p.sb(f"o{i}", [8, 512]) for i in range(2)]
        for j in range(9):
            wt, bt, po, ot = wts[j % 2], bts[j % 2], pos[j % 2], ots[j % 2]
            p.dma(wt, w[:, j * 512:(j + 1) * 512].rearrange("(k p) n -> p k n", p=128))
            p.dma(bt, b[:, j * 512:(j + 1) * 512].partition_broadcast(8), eng="act")
            for k in range(8):
                p.mm(po[0:8, :], scT[:, k, :], wt[:, k, :], start=(k == 0), stop=(k == 7))
            p.tt(ot, po[0:8, :], bt, ALU.add)
            p.dma(out[:, j * 512:(j + 1) * 512], ot)
        p.emit()
    return nc


def run_mod(inp):
    cc = np.zeros((8, D), np.float32)
    cc[0:4] = inp["c"]
    cc[4] = inp["c_ctx"]
    in_maps = []
    for core in range(8):
        l, h = core // 2, core % 2
        in_maps.append({
            "cc": cc,
            "w": np.ascontiguousarray(inp["ada_w"][l][:, h * 4608:(h + 1) * 4608]),
            "b": np.ascontiguousarray(inp["ada_b"][l][None, h * 4608:(h + 1) * 4608]),
        })
    res = run_bass_kernel_spmd(build_mod(), in_maps, core_ids=list(range(8)))
    mod = np.zeros((DEPTH, 5, NMOD, D), np.float32)
    for core in range(8):
        l, h = core // 2, core % 2
        m = res.results[core]["mod"][0:5]
        mod[l].reshape(5, NMOD * D)[:, h * 4608:(h + 1) * 4608] = m
    return mod

GT = 256
RMS_EPS = 1e-6


class TCtx:
    ARENA = 50500

    def __init__(self, p, nc):
        self.p = p
        self.nc = nc
        self.ident = make_ident(p)
        self.eps = p.sb("eps_c", [128, 1])
        p.memset(self.eps, RMS_EPS)
        self.arena = p.sb("arena", [128, self.ARENA])
        self.pT = [p.ps(f"pT{i}", [128, 512]) for i in range(2)]
        self.pA = [p.ps(f"pA{i}", [128, 512]) for i in range(4)]
        self.pO = [p.ps(f"pO{i}", [128, 512]) for i in range(2)]
        self.npass = 0
        self.off = 0

    def alloc(self, name, shape, dtype=F32):
        p = self.p
        per = 1
        for s_ in shape[1:]:
            per *= s_
        nbytes = per * (2 if dtype == BF16 else 4)
        n4 = (nbytes + 3) // 4
        assert self.off + n4 <= self.ARENA, (name, self.off, n4)
        ap = self.arena.t[0:shape[0], self.off:self.off + n4]
        self.off += n4
        if dtype == BF16:
            ap = ap.bitcast(BF16)
            ap = ap[:, 0:per]
        if len(shape) == 3:
            ap = ap.rearrange("p (a b) -> p a b", a=shape[1])
        t = Tile(p, f"{name}_{self.npass}", shape, dtype, "sb")
        t.t = ap
        p.tiles.append(t)
        return t

    def begin_pass(self, wA_cols=0, wB_k=0):
        p = self.p
        if self.npass > 0:
            p.barrier()
        self.npass += 1
        self.off = 0
        if wA_cols:
            self.wA = self.alloc("wA", [128, 8, wA_cols], BF16)
        if wB_k:
            self.wB = self.alloc("wB", [128, wB_k, 1024], BF16)
        self.Gp = self.alloc("Gp", [128, D])
        self.SH = self.alloc("SH", [128, D])
        self.GA = self.alloc("GA", [128, D])
        self.xt = [self.alloc(f"xt{i}", [128, D]) for i in range(4)]
        self.h = [self.alloc(f"h{i}", [128, D]) for i in range(2)]
        self.junk = self.alloc("junk", [128, D], BF16)
        self.hT4 = [self.alloc(f"hT{i}", [128, 2, GT], BF16) for i in range(4)]
        self.ss = [self.alloc(f"ss{i}", [128, 1]) for i in range(2)]
        self.rs = [self.alloc(f"rs{i}", [128, 1]) for i in range(2)]
        self.tmp = [self.alloc(f"tmp{i}", [128, 512]) for i in range(2)]

    def load_w(self, dst_view, src_ap):
        K = src_ap.shape[0] // 128
        sv = src_ap.rearrange("(k p) n -> p k n", p=128)
        for k in range(K):
            self.p.dma(dst_view[:, k, :], sv[:, k, :], eng="pool", max_dma_last_dim=8192)

    def load_mod(self, mod_d, g_row, j, gate_scale):
        p = self.p
        p.dma(self.SH, mod_d[3 * j:3 * j + 1, :].partition_broadcast(128))
        p.dma(self.Gp, mod_d[3 * j + 1:3 * j + 2, :].partition_broadcast(128))
        p.dma(self.GA, mod_d[3 * j + 2:3 * j + 3, :].partition_broadcast(128))
        gt = self.h[0]
        p.dma(gt, g_row.partition_broadcast(128))
        p.stt(self.Gp, self.Gp, 1.0, gt, ALU.add, ALU.mult)
        if gate_scale != 1.0:
            p.ts(self.GA, self.GA, float(gate_scale), None, op0=ALU.mult, eng="pool")

    def norm_mod(self, xt, hi):
        p = self.p
        ss, rs, h = self.ss[hi], self.rs[hi], self.h[hi]
        p.act(self.junk, xt, AF.Square, accum_out=ss)
        p.act(ss, ss, AF.Sqrt, scale=1.0 / D, bias=self.eps)
        rsa, ssa = rs.t[:], ss.t[:]
        p.op("dve", lambda e: e.reciprocal(rsa, ssa), [ss], [rs])
        p.stt(h, xt, rs[:, 0:1], self.Gp, ALU.mult, ALU.mult)
        p.tt(h, h, self.SH, ALU.add, eng="pool")
        return h

    def to_hT(self, hs, nt):
        p = self.p
        ntile = len(hs)
        for kk in range(4):
            pt = self.pT[kk % 2]
            for kl in range(2):
                k = kk * 2 + kl
                for t in range(ntile):
                    p.transpose(pt[:, kl * GT + t * 128: kl * GT + (t + 1) * 128], hs[t][:, k * 128:(k + 1) * 128], self.ident)
            if ntile == 2:
                p.copy(self.hT4[kk].v.rearrange("p a b -> p (a b)"), pt, eng="act")
            else:
                for kl in range(2):
                    p.copy(self.hT4[kk][:, kl, 0:128], pt[:, kl * GT: kl * GT + 128], eng="act")

    def hTk(self, k, nt):
        return self.hT4[k // 2][:, k % 2, 0:nt]


def token_groups(NL, NC):
    gs = []
    for g in range(NL // GT):
        gs.append((g * GT, 2, False))
    r = NL
    while r < NL + NC:
        n = min(2, (NL + NC - r) // 128)
        gs.append((r, n, True))
        r += n * 128
    return gs


def run_groups(T, x_src, x_dst, NL, NC, modl, modc, g_row, j, gate_scale, body):
    p = T.p
    cur_ctx = None
    for gi, (r0, ntile, is_ctx) in enumerate(token_groups(NL, NC)):
        if cur_ctx != is_ctx:
            T.load_mod(modc if is_ctx else modl, g_row, j, gate_scale)
            cur_ctx = is_ctx
        xts = [T.xt[(gi % 2) * 2 + t] for t in range(ntile)]
        for t in range(ntile):
            p.dma(xts[t], x_src[r0 + t * 128: r0 + (t + 1) * 128, :])
        body(r0, ntile, is_ctx, xts)
        if x_dst is not None:
            for t in range(ntile):
                p.dma(x_dst[r0 + t * 128: r0 + (t + 1) * 128, :], xts[t], eng="act")


def resid_add(T, xt, po, col0, ncol, ti):
    p = T.p
    tmp = T.tmp[ti % 2]
    p.tt(tmp[:, 0:ncol], po[:, 0:ncol], T.GA[:, col0:col0 + ncol], ALU.mult)
    p.tt(xt[:, col0:col0 + ncol], xt[:, col0:col0 + ncol], tmp[:, 0:ncol], ALU.add, eng="pool")


def ffn_pass(T, x_src, x_dst, NL, NC, wi_d, wo_d, modl, modc, g_row, j):
    p = T.p
    T.begin_pass(wA_cols=5632, wB_k=22)
    T.aT = [T.alloc(f"aT{i}", [128, GT], BF16) for i in range(22)]
    T.sg = [T.alloc(f"sg{i}", [128, GT]) for i in range(2)]
    T.load_w(T.wA, wi_d)
    T.load_w(T.wB, wo_d)
    wi, wo = T.wA, T.wB
    st = {"c": 0}

    def body(r0, ntile, is_ctx, xts):
        nt = ntile * 128
        hs = [T.norm_mod(xts[t], t) for t in range(ntile)]
        T.to_hT(hs, nt)
        for jj in range(22):
            c = st["c"]
            st["c"] += 1
            pG, pU = T.pA[(c % 2) * 2], T.pA[(c % 2) * 2 + 1]
            for k in range(8):
                p.mm(pG[:, 0:nt], wi[:, k, jj * 128:(jj + 1) * 128], T.hTk(k, nt), start=(k == 0), stop=(k == 7))
            for k in range(8):
                p.mm(pU[:, 0:nt], wi[:, k, DFF + jj * 128:DFF + (jj + 1) * 128], T.hTk(k, nt), start=(k == 0), stop=(k == 7))
            sg = T.sg[c % 2]
            p.act(sg[:, 0:nt], pG[:, 0:nt], AF.Silu)
            p.tt(T.aT[jj][:, 0:nt], sg[:, 0:nt], pU[:, 0:nt], ALU.mult)
        for t in range(ntile):
            for nh in range(2):
                c = st["c"]
                st["c"] += 1
                po = T.pO[c % 2]
                for jj in range(22):
                    p.mm(po, T.aT[jj][:, t * 128:(t + 1) * 128], wo[:, jj, nh * 512:(nh + 1) * 512],
                         start=(jj == 0), stop=(jj == 21))
                resid_add(T, xts[t], po, nh * 512, 512, c)

    run_groups(T, x_src, x_dst, NL, NC, modl, modc, g_row, j, 0.5, body)

GELU_C = 1.5957691216057308


def s5pre_pass(T, x_src, u_dst, NL, NC, modl, modc, g_row):
    p = T.p
    T.begin_pass()

    def body(r0, ntile, is_ctx, xts):
        for t in range(ntile):
            h = T.norm_mod(xts[t], t)
            p.dma(u_dst[r0 + t * 128: r0 + (t + 1) * 128, :], h, eng="act")

    run_groups(T, x_src, None, NL, NC, modl, modc, g_row, 1, 1.0, body)


def head_rms(T, src, nh, dst, gains):
    p = T.p
    w = nh * 128
    sq = T.hs_sq
    p.act(sq[:, 0:w], src[:, 0:w], AF.Square)
    st = T.hs_st
    p.reduce(st[:, 0:nh], sq[:, 0:w].rearrange("p (h d) -> p h d", d=128), ALU.add)
    p.act(st[:, 0:nh], st[:, 0:nh], AF.Sqrt, scale=1.0 / 128, bias=T.eps)
    sta = st.t[:, 0:nh]
    p.op("dve", lambda e: e.reciprocal(sta, sta), [st], [st])
    bc = View(st, st.t[:, 0:nh].unsqueeze(2).broadcast_to([128, nh, 128]))
    p.tt(dst[:, 0:w].rearrange("p (h d) -> p h d", d=128), src[:, 0:w].rearrange("p (h d) -> p h d", d=128), bc, ALU.mult)
    p.tt(dst[:, 0:w], dst[:, 0:w], gains[:, 0:w], ALU.mult, eng="pool")


def alloc_headstuff(T):
    T.hs_sq = T.alloc("hs_sq", [128, 1280])
    T.hs_st = T.alloc("hs_st", [128, 16])
    T.gains = T.alloc("gains", [128, 1280])


def attpre_pass(T, x_src, qkv_dst, NL, NC, modl, modc, g_row, wqkv_d, qg_d, kg_d, ropeC_d, ropeS_d):
    p = T.p
    T.begin_pass(wA_cols=1536)
    alloc_headstuff(T)
    T.load_w(T.wA, wqkv_d)
    w = T.wA
    for hh in range(8):
        p.dma(T.gains[:, hh * 128:(hh + 1) * 128], qg_d.partition_broadcast(128))
    for hh in range(2):
        p.dma(T.gains[:, 1024 + hh * 128:1024 + (hh + 1) * 128], kg_d.partition_broadcast(128))
    qkv = T.alloc("qkv_sb", [128, 1536])
    qn = T.alloc("qn_sb", [128, 1280])
    ra = T.alloc("ropeA", [128, 1280])
    rb = T.alloc("ropeB", [128, 1280])
    ct = T.alloc("ropeC", [128, 128])
    sn = T.alloc("ropeS", [128, 128])
    ob = T.alloc("qkv_o", [128, 1536], BF16)
    st = {"c": 0}

    def body(r0, ntile, is_ctx, xts):
        nt = ntile * 128
        hs = [T.norm_mod(xts[t], t) for t in range(ntile)]
        T.to_hT(hs, nt)
        for t in range(ntile):
            for cb in range(3):
                c = st["c"]
                st["c"] += 1
                po = T.pA[c % 4]
                for k in range(8):
                    p.mm(po, T.hT4[k // 2][:, k % 2, t * 128:(t + 1) * 128], w[:, k, cb * 512:(cb + 1) * 512],
                         start=(k == 0), stop=(k == 7))
                p.copy(qkv[:, cb * 512:(cb + 1) * 512], po, eng="act")
            head_rms(T, qkv, 10, qn, T.gains)
            if is_ctx:
                p.copy(ob[:, 0:1280], qn, eng="pool")
            else:
                row = r0 + t * 128
                p.dma(ct, ropeC_d[row:row + 128, :])
                p.dma(sn, ropeS_d[row:row + 128, :])
                q3 = qn.v.rearrange("p (h d) -> p h d", d=128)
                cb_ = View(ct, ct.t[:].unsqueeze(1).broadcast_to([128, 10, 128]))
                p.tt(ra.v.rearrange("p (h d) -> p h d", d=128), q3, cb_, ALU.mult)
                q5 = qn.v.rearrange("p (h a f j) -> p h a f j", a=2, f=2, j=32)
                b5 = rb.v.rearrange("p (h a f j) -> p h a f j", a=2, f=2, j=32)
                s4 = sn.t[:].rearrange("p (a f j) -> p a f j", a=2, f=2)
                for f in range(2):
                    sb_ = View(sn, s4[:, :, f, :].unsqueeze(1).broadcast_to([128, 10, 2, 32]))
                    p.tt(b5[:, :, :, f, :], q5[:, :, :, 1 - f, :], sb_, ALU.mult, eng="pool")
                p.tt(ob[:, 0:1280], ra, rb, ALU.add)
            p.copy(ob[:, 1280:1536], qkv[:, 1280:1536], eng="act")
            p.dma(qkv_dst[r0 + t * 128: r0 + (t + 1) * 128, :], ob, eng="act")

    run_groups(T, x_src, None, NL, NC, modl, modc, g_row, 1, 1.0, body)


def proj_resid_pass(T, x_src, x_dst, NL, NC, modl, modc, g_row, aT_d, w_d):
    p = T.p
    T.begin_pass(wB_k=8)
    T.load_w(T.wB, w_d)
    w = T.wB
    av = aT_d.rearrange("(k p) n -> p k n", p=128)
    st = {"c": 0}

    def body(r0, ntile, is_ctx, xts):
        nt = ntile * 128
        for kk in range(4):
            p.dma(T.hT4[kk][:, :, 0:nt], av[:, kk * 2:kk * 2 + 2, r0:r0 + nt])
        for t in range(ntile):
            for nh in range(2):
                c = st["c"]
                st["c"] += 1
                po = T.pO[c % 2]
                for k in range(8):
                    p.mm(po, T.hT4[k // 2][:, k % 2, t * 128:(t + 1) * 128], w[:, k, nh * 512:(nh + 1) * 512],
                         start=(k == 0), stop=(k == 7))
                resid_add(T, xts[t], po, nh * 512, 512, c)

    run_groups(T, x_src, x_dst, NL, NC, modl, modc, g_row, 1, 1.0, body)


def s5post_pass(T, x_src, x_dst, NL, NC, modl, modc, g_row, yT_d, wglu_d):
    p = T.p
    T.begin_pass(wA_cols=2048)
    T.load_w(T.wA, wglu_d)
    w = T.wA
    yv = yT_d.rearrange("(k p) n -> p k n", p=128)
    yin = [T.alloc(f"yin{i}", [128, 2, GT]) for i in range(2)]
    t1 = [T.alloc(f"gl_t{i}", [128, 2, GT]) for i in range(2)]
    sig = T.alloc("glu_sig", [128, 512])
    st = {"c": 0}

    def body(r0, ntile, is_ctx, xts):
        nt = ntile * 128
        for kk in range(4):
            yi, tt_ = yin[kk % 2], t1[kk % 2]
            p.dma(yi[:, :, 0:nt], yv[:, kk * 2:kk * 2 + 2, r0:r0 + nt])
            a, b = yi[:, :, 0:nt], tt_[:, :, 0:nt]
            p.tt(b, a, a, ALU.mult, eng="pool")
            p.ts(b, b, 0.044715, 1.0, op0=ALU.mult, op1=ALU.add)
            p.tt(b, b, a, ALU.mult, eng="pool")
            p.act(b, b, AF.Sigmoid, scale=GELU_C)
            p.tt(T.hT4[kk][:, :, 0:nt], a, b, ALU.mult)
        for t in range(ntile):
            for nh in range(2):
                c = st["c"]
                st["c"] += 1
                pv, pg = T.pA[(c % 2) * 2], T.pA[(c % 2) * 2 + 1]
                for k in range(8):
                    p.mm(pv, T.hT4[k // 2][:, k % 2, t * 128:(t + 1) * 128], w[:, k, nh * 512:(nh + 1) * 512],
                         start=(k == 0), stop=(k == 7))
                for k in range(8):
                    p.mm(pg, T.hT4[k // 2][:, k % 2, t * 128:(t + 1) * 128], w[:, k, 1024 + nh * 512:1024 + (nh + 1) * 512],
                         start=(k == 0), stop=(k == 7))
                p.act(sig, pg, AF.Sigmoid)
                tmp = T.tmp[c % 2]
                p.tt(tmp, pv, sig, ALU.mult)
                p.tt(tmp, tmp, T.GA[:, nh * 512:(nh + 1) * 512], ALU.mult, eng="pool")
                p.tt(xts[t][:, nh * 512:(nh + 1) * 512], xts[t][:, nh * 512:(nh + 1) * 512], tmp, ALU.add)

    run_groups(T, x_src, x_dst, NL, NC, modl, modc, g_row, 1, 1.0, body)


def gdnpre_pass(T, x_src, proj_dst, NL, NC, modl, modc, g_row, win_d):
    p = T.p
    T.begin_pass(wA_cols=4128)
    T.load_w(T.wA, win_d)
    w = T.wA
    pr = [T.alloc(f"proj_sb{i}", [128, 4128]) for i in range(1)]
    st = {"c": 0}

    def body(r0, ntile, is_ctx, xts):
        nt = ntile * 128
        hs = [T.norm_mod(xts[t], t) for t in range(ntile)]
        T.to_hT(hs, nt)
        for t in range(ntile):
            prt = pr[0]
            for cb in range(9):
                ncol = 512 if cb < 8 else 32
                c = st["c"]
                st["c"] += 1
                po = T.pA[c % 4]
                for k in range(8):
                    p.mm(po[:, 0:ncol], T.hT4[k // 2][:, k % 2, t * 128:(t + 1) * 128], w[:, k, cb * 512:cb * 512 + ncol],
                         start=(k == 0), stop=(k == 7))
                p.copy(prt[:, cb * 512:cb * 512 + ncol], po[:, 0:ncol], eng=("act" if cb % 2 == 0 else "dve"))
            p.dma(proj_dst[r0 + t * 128: r0 + (t + 1) * 128, :], prt, eng="act")

    run_groups(T, x_src, None, NL, NC, modl, modc, g_row, 1, 1.0, body)


def gdnpost_pass(T, x_src, x_dst, NL, NC, modl, modc, g_row, o_d, z_d, og_d, wo_d):
    p = T.p
    T.begin_pass(wB_k=8)
    alloc_headstuff(T)
    T.load_w(T.wB, wo_d)
    w = T.wB
    for hh in range(8):
        p.dma(T.gains[:, hh * 128:(hh + 1) * 128], og_d.partition_broadcast(128))
    ot = [T.alloc(f"go_t{i}", [128, D]) for i in range(2)]
    zt = [T.alloc(f"gz_t{i}", [128, D]) for i in range(2)]
    st = {"c": 0}

    def body(r0, ntile, is_ctx, xts):
        nt = ntile * 128
        ys = []
        for t in range(ntile):
            row = r0 + t * 128
            p.dma(ot[t], o_d[0, row:row + 128, :])
            p.dma(zt[t], o_d[1, row:row + 128, :])
            p.tt(ot[t], ot[t], zt[t], ALU.add, eng="pool")
            p.dma(zt[t], z_d[row:row + 128, :])
            head_rms(T, ot[t], 8, ot[t], T.gains)
            p.act(zt[t], zt[t], AF.Silu)
            p.tt(ot[t], ot[t], zt[t], ALU.mult)
            ys.append(ot[t])
        T.to_hT(ys, nt)
        for t in range(ntile):
            for nh in range(2):
                c = st["c"]
                st["c"] += 1
                po = T.pO[c % 2]
                for k in range(8):
                    p.mm(po, T.hT4[k // 2][:, k % 2, t * 128:(t + 1) * 128], w[:, k, nh * 512:(nh + 1) * 512],
                         start=(k == 0), stop=(k == 7))
                resid_add(T, xts[t], po, nh * 512, 512, c)

    run_groups(T, x_src, x_dst, NL, NC, modl, modc, g_row, 1, 1.0, body)

def rope_tables(L):
    t = np.arange(L)
    row = (t // 64).astype(np.float32)
    col = (t % 64).astype(np.float32)
    inv = (np.float32(10000.0) ** (-np.arange(0, 64, 2, dtype=np.float32) / np.float32(64))).astype(np.float32)
    ar = row[:, None] * inv[None, :]
    ac = col[:, None] * inv[None, :]
    cr, sr, cc, sc = np.cos(ar), np.sin(ar), np.cos(ac), np.sin(ac)
    C = np.concatenate([cr, cr, cc, cc], 1).astype(np.float32)
    S = np.concatenate([-sr, sr, -sc, sc], 1).astype(np.float32)
    return C, S

def build_att(NLAT, NCTX):
    NQ = NLAT + NCTX
    NK = NCTX + NLAT
    nkt = NK // 128
    nct = NCTX // 128
    nc = bass.Bass("TRN2", target_bir_lowering=False)
    qT_d = nc.dram_tensor("qT", [4, 128, NQ], BF16, kind="ExternalInput").ap()
    kT_d = nc.dram_tensor("kT", [128, NK], BF16, kind="ExternalInput").ap()
    v_d = nc.dram_tensor("v", [NK, 128], BF16, kind="ExternalInput").ap()
    qg_d = nc.dram_tensor("qg", [1, 128], F32, kind="ExternalInput").ap()
    kg_d = nc.dram_tensor("kg", [1, 128], F32, kind="ExternalInput").ap()
    oT_d = nc.dram_tensor("oT", [4, 128, NQ], BF16, kind="ExternalOutput").ap()
    scale = 128.0 ** -0.5
    with ExitStack() as st:
        p = Prog(nc, st)
        qT = [p.sb(f"qTs{g}", [128, NQ], BF16) for g in range(4)]
        kT = p.sb("kTs", [128, NK], BF16)
        V = p.sb("Vs", [128, nkt, 128], BF16)
        ones = p.sb("ones", [128, 128], BF16)
        onesf = p.sb("onesf", [1, 128])
        gq = p.sb("gqs", [1, 128])
        gk = p.sb("gks", [1, 128])
        mq = p.sb("mq", [1, 1])
        mk = p.sb("mk", [1, 1])
        negb = p.sb("negb", [128, 1])
        pts = [p.sb(f"pt{i}", [128, 512], BF16) for i in range(4)]
        rl = [p.sb(f"rl{i}", [128, 512]) for i in range(2)]
        ob = [p.sb(f"ob{i}", [128, 512], BF16) for i in range(2)]
        pS = [p.ps(f"pS{i}", [128, 512]) for i in range(3)]
        pO = [p.ps(f"pO{i}", [128, 512]) for i in range(2)]
        pL = [p.ps(f"pL{i}", [128, 512]) for i in range(2)]
        for g in range(4):
            p.dma(qT[g], qT_d[g])
        p.dma(kT, kT_d)
        p.dma(V, v_d.rearrange("(t p) d -> p t d", p=128))
        p.dma(gq, qg_d)
        p.dma(gk, kg_d)
        p.memset(ones, 1.0)
        p.memset(onesf, 1.0)
        gqa, gka, mqa, mka = gq.t[:], gk.t[:], mq.t[:], mk.t[:]
        p.act(gq, gq, AF.Abs)
        p.act(gk, gk, AF.Abs)
        p.op("dve", lambda e: e.tensor_reduce(mqa, gqa, AX.X, ALU.max), [gq], [mq])
        p.op("dve", lambda e: e.tensor_reduce(mka, gka, AX.X, ALU.max), [gk], [mk])
        p.tt(mq, mq, mk, ALU.mult)
        p.mm(pS[0][:, 0:1], onesf, mq, start=True, stop=True)
        p.ts(negb, pS[0][:, 0:1], -(128.0 ** 0.5), None, op0=ALU.mult)
        cnt = {"s": 0, "o": 0}

        def block(q0, nq, kts):
            for g in range(4):
                o = cnt["o"]
                cnt["o"] += 1
                po, pl = pO[o % 2], pL[o % 2]
                for i, kt in enumerate(kts):
                    s = cnt["s"]
                    cnt["s"] += 1
                    ps_, pt = pS[s % 3], pts[s % 4]
                    p.mm(ps_[:, 0:nq], kT[:, kt * 128:(kt + 1) * 128], qT[g][:, q0:q0 + nq], start=True, stop=True)
                    p.act(pt[:, 0:nq], ps_[:, 0:nq], AF.Exp, bias=negb, scale=scale)
                    p.mm(po[:, 0:nq], V[:, kt, :], pt[:, 0:nq], start=(i == 0), stop=(i == len(kts) - 1))
                    p.mm(pl[:, 0:nq], ones, pt[:, 0:nq], start=(i == 0), stop=(i == len(kts) - 1))
                r, ot = rl[o % 2], ob[o % 2]
                ra, pla = r.t[:, 0:nq], pl.t[:, 0:nq]
                p.op("dve", lambda e, ra=ra, pla=pla: e.reciprocal(ra, pla), [pl], [r])
                p.tt(ot[:, 0:nq], po[:, 0:nq], r[:, 0:nq], ALU.mult)
                p.dma(oT_d[g][:, q0:q0 + nq], ot[:, 0:nq])

        for qb in range(NLAT // 512):
            block(qb * 512, 512, list(range(nkt)))
        block(NLAT, NCTX, list(range(nct)))
        p.emit()
    return nc

import math
TWO_PI = 2.0 * math.pi


class Arena:
    def __init__(self, p, name, size):
        self.p = p
        self.size = size
        self.t = p.sb(name, [128, size])
        self.off = 0
        self.n = 0

    def alloc(self, name, shape, dtype=F32):
        per = 1
        for s_ in shape[1:]:
            per *= s_
        nbytes = per * (2 if dtype == BF16 else 4)
        n4 = (nbytes + 3) // 4
        assert self.off + n4 <= self.size, (name, self.off, n4, self.size)
        ap = self.t.t[0:shape[0], self.off:self.off + n4]
        self.off += n4
        if dtype == BF16:
            ap = ap.bitcast(BF16)[:, 0:per]
        if len(shape) == 3:
            ap = ap.rearrange("p (a b) -> p a b", a=shape[1])
        self.n += 1
        t = Tile(self.p, f"{name}_{self.n}", shape, dtype, "sb")
        t.t = ap
        self.p.tiles.append(t)
        return t


def s5_abar(p, A, lr, li, ldt, shape, tag):
    dt = A.alloc(f"dt{tag}", shape)
    p.act(dt, ldt, AF.Exp)
    zr = A.alloc(f"zr{tag}", shape)
    ph = A.alloc(f"ph{tag}", shape)
    pc = A.alloc(f"pc{tag}", shape)
    m = A.alloc(f"m{tag}", shape)
    p.tt(zr, lr, dt, ALU.mult)
    p.tt(ph, li, dt, ALU.mult)
    rho = A.alloc(f"rho{tag}", shape)
    p.act(rho, zr, AF.Exp)
    for _ in range(5):
        p.ts(m, ph, math.pi, None, op0=ALU.is_gt)
        p.stt(ph, m, -TWO_PI, ph, ALU.mult, ALU.add)
    p.ts(pc, ph, math.pi / 2, None, op0=ALU.add)
    p.ts(m, pc, math.pi, None, op0=ALU.is_gt)
    p.stt(pc, m, -TWO_PI, pc, ALU.mult, ALU.add)
    sn = A.alloc(f"sn{tag}", shape)
    cs = A.alloc(f"cs{tag}", shape)
    p.act(sn, ph, AF.Sin)
    p.act(cs, pc, AF.Sin)
    return rho, cs, sn


def build_s5(NLAT, NCTX, NB=4):
    NTOK = NCTX + NLAT
    TC = 512
    nc = bass.Bass("TRN2", target_bir_lowering=False)
    u_d = nc.dram_tensor("uT", [128, NB, NTOK], F32, kind="ExternalInput").ap()
    lamc_d = nc.dram_tensor("lam_c", [128, 2, 16], F32, kind="ExternalInput").ap()
    ldtc_d = nc.dram_tensor("ldt_c", [128, 16], F32, kind="ExternalInput").ap()
    lamr_d = nc.dram_tensor("lam_r", [128, 2, 1024], F32, kind="ExternalInput").ap()
    ldtr_d = nc.dram_tensor("ldt_r", [128, 1024], F32, kind="ExternalInput").ap()
    bT_d = nc.dram_tensor("bT", [2, 16, 16, 64], F32, kind="ExternalInput").ap()
    cT_d = nc.dram_tensor("cT", [2, 16, 64, 16], F32, kind="ExternalInput").ap()
    d_d = nc.dram_tensor("dskip", [128, 1], F32, kind="ExternalInput").ap()
    y_d = nc.dram_tensor("yT", [128, NB, NTOK], F32, kind="ExternalOutput").ap()
    with ExitStack() as st:
        p = Prog(nc, st)
        ident = make_ident(p)
        A = Arena(p, "s5arena", 49000)
        pA = [p.ps(f"pA{i}", [128, 512]) for i in range(2)]
        pB = [p.ps(f"pB{i}", [128, 512]) for i in range(2)]
        pY = [p.ps(f"pY{i}", [128, 512]) for i in range(2)]
        pR = p.ps("pR", [128, 512])
        COS = A.alloc("COS", [128, 16, 520])
        SIN = A.alloc("SIN", [128, 16, 520])
        LB = A.alloc("LB", [128, 16, 128], BF16)
        LBs = A.alloc("LBs", [128, 16, 128], BF16)
        L1 = A.alloc("L1", [128, 16, 128], BF16)
        L2 = A.alloc("L2", [128, 16, 128], BF16)
        R5 = A.alloc("R5", [128, 16, 128])
        R2 = A.alloc("R2", [128, 16, 128])
        SW = A.alloc("SW", [128, 128])
        dcol = A.alloc("dcol", [128, 1])
        rho_c = A.alloc("rho_cc", [128, 16])
        ST = A.alloc("ST", [128, 16])
        mark = A.off
        lamc = A.alloc("lamc", [128, 2, 16])
        ldtc = A.alloc("ldtc", [128, 16])
        p.dma(lamc, lamc_d)
        p.dma(ldtc, ldtc_d)
        p.dma(dcol, d_d)
        rho_t, cs_c, sn_c = s5_abar(p, A, lamc[:, 0, :], lamc[:, 1, :], ldtc, [128, 16], "c")
        p.copy(rho_c, rho_t)
        p.memset(COS[:, :, 0:1], 1.0)
        p.memset(SIN[:, :, 0:1], 0.0)
        p.copy(COS[:, :, 1:2], View(cs_c, cs_c.t[:].unsqueeze(2)))
        p.copy(SIN[:, :, 1:2], View(sn_c, sn_c.t[:].unsqueeze(2)))
        t1 = A.alloc("tb1", [128, 16, 256])
        t2 = A.alloc("tb2", [128, 16, 256])
        for k in range(9):
            n = 1 << k
            src = slice(1, n + 1)
            dst = slice(n + 1, 2 * n + 1)
            ck = View(COS, COS.t[:, :, n:n + 1].broadcast_to([128, 16, n]))
            sk = View(SIN, SIN.t[:, :, n:n + 1].broadcast_to([128, 16, n]))
            p.tt(t1[:, :, 0:n], COS[:, :, src], ck, ALU.mult)
            p.tt(t2[:, :, 0:n], SIN[:, :, src], sk, ALU.mult, eng="pool")
            p.tt(COS[:, :, dst], t1[:, :, 0:n], t2[:, :, 0:n], ALU.subtract)
            p.tt(t1[:, :, 0:n], SIN[:, :, src], ck, ALU.mult)
            p.tt(t2[:, :, 0:n], COS[:, :, src], sk, ALU.mult, eng="pool")
            p.tt(SIN[:, :, dst], t1[:, :, 0:n], t2[:, :, 0:n], ALU.add)
        p.memset(SW, 0.0, eng="pool")
        swa = SW.t[:]
        p.op("pool", lambda e: e.affine_select(swa, swa, [[-1, 128]], ALU.not_equal, 1.0, base=64, channel_multiplier=1), [SW], [SW])
        p.op("pool", lambda e: e.affine_select(swa, swa, [[-1, 128]], ALU.not_equal, -1.0, base=-64, channel_multiplier=1), [SW], [SW])
        tmpR = A.alloc("tmpR", [128, 128])
        for (Rm, Tn) in ((R5, 512), (R2, NCTX)):
            for gr in range(16):
                p.ts(tmpR, SW, SIN[:, gr, Tn:Tn + 1], None, op0=ALU.mult)
                p.stt(Rm[:, gr, :], ident, COS[:, gr, Tn:Tn + 1], tmpR, ALU.mult, ALU.add)
        p.ts(SIN[64:128, :, :], SIN[64:128, :, :], -1.0, None, op0=ALU.mult)
        p.barrier()
        A.off = mark
        lamr = A.alloc("lamr", [128, 2, 1024])
        ldtr = A.alloc("ldtr", [128, 1024])
        p.dma(lamr, lamr_d)
        p.dma(ldtr, ldtr_d)
        lr, li = lamr[:, 0, :], lamr[:, 1, :]
        rho_r, cs_r, sn_r = s5_abar(p, A, lr, li, ldtr, [128, 1024], "r")
        ar = A.alloc("ar", [128, 1024])
        ai = A.alloc("ai", [128, 1024])
        den = A.alloc("den", [128, 1024])
        tq = A.alloc("tq", [128, 1024])
        fr = A.alloc("fr", [128, 1024])
        fi = A.alloc("fi", [128, 1024])
        p.tt(ar, rho_r, cs_r, ALU.mult)
        p.ts(ar, ar, -1.0, None, op0=ALU.add)
        p.tt(ai, rho_r, sn_r, ALU.mult)
        p.tt(den, lr, lr, ALU.mult)
        p.tt(tq, li, li, ALU.mult)
        p.tt(den, den, tq, ALU.add)
        dena = den.t[:]
        p.op("dve", lambda e: e.reciprocal(dena, dena), [den], [den])
        p.tt(fr, ar, lr, ALU.mult)
        p.tt(tq, ai, li, ALU.mult)
        p.tt(fr, fr, tq, ALU.add)
        p.tt(fr, fr, den, ALU.mult)
        p.tt(fi, ai, lr, ALU.mult)
        p.tt(tq, ar, li, ALU.mult)
        p.tt(fi, fi, tq, ALU.subtract)
        p.tt(fi, fi, den, ALU.mult)
        BTr = A.alloc("BTr", [128, 16, 64])
        BTi = A.alloc("BTi", [128, 16, 64])
        p.memset(BTr, 0.0)
        p.memset(BTi, 0.0, eng="pool")
        for gr in range(16):
            g = gr % 8
            p.dma(BTr[16 * g:16 * g + 16, gr, :], bT_d[0, gr])
            p.dma(BTi[16 * g:16 * g + 16, gr, :], bT_d[1, gr])
        f3r = fr.v.rearrange("p (a b) -> p a b", a=16)
        f3i = fi.v.rearrange("p (a b) -> p a b", a=16)
        q1 = A.alloc("q1", [128, 16, 64])
        q2 = A.alloc("q2", [128, 16, 64])
        p.tt(q1, f3r, BTr, ALU.mult)
        p.tt(q2, f3i, BTi, ALU.mult, eng="pool")
        p.tt(LB[:, :, 0:64], q1, q2, ALU.subtract)
        p.tt(LBs[:, :, 64:128], q1, q2, ALU.subtract)
        p.tt(q1, f3r, BTi, ALU.mult)
        p.tt(q2, f3i, BTr, ALU.mult, eng="pool")
        p.tt(LB[:, :, 64:128], q1, q2, ALU.add)
        p.tt(LBs[:, :, 0:64], q1, q2, ALU.add)
        CR = A.alloc("CR", [128, 16, 16])
        CI = A.alloc("CI", [128, 16, 16])
        for half in range(2):
            p.dma(CR[64 * half:64 * half + 64, :, :], cT_d[0].rearrange("g p h -> p g h"))
            p.dma(CI[64 * half:64 * half + 64, :, :], cT_d[1].rearrange("g p h -> p g h"))
        p.memset(L1, 0.0)
        p.memset(L2, 0.0, eng="pool")
        for gr in range(16):
            g = gr % 8
            cs_ = slice(16 * g, 16 * g + 16)
            p.copy(L1[0:64, gr, cs_], CR[0:64, gr, :])
            p.ts(L1[64:128, gr, cs_], CI[64:128, gr, :], -1.0, None, op0=ALU.mult)
            p.ts(L2[0:64, gr, cs_], CI[0:64, gr, :], -1.0, None, op0=ALU.mult)
            p.copy(L2[64:128, gr, cs_], CR[64:128, gr, :])
        p.barrier()
        A.off = mark
        yacc = A.alloc("yacc", [128, NTOK])
        NW = 3
        ub = [A.alloc(f"ub{i}", [128, 512], BF16) for i in range(NW)]
        uf = [A.alloc(f"uf{i}", [128, 512]) for i in range(NW)]
        m1 = [A.alloc(f"m1_{i}", [128, 512]) for i in range(NW)]
        m2 = [A.alloc(f"m2_{i}", [128, 512]) for i in range(NW)]
        gs = [A.alloc(f"gs{i}", [128, 512]) for i in range(NW)]
        G1 = [A.alloc(f"G1_{i}", [128, 512], BF16) for i in range(NW)]
        G2 = [A.alloc(f"G2_{i}", [128, 512], BF16) for i in range(NW)]
        cnt = {"w": 0, "c": 0}
        for b in range(NB):
            for r in range(2):
                chunks = [(0, NCTX, True)] + [(NCTX + c * TC, TC, False) for c in range(NLAT // TC)]
                if r == 1:
                    chunks = [chunks[0]] + chunks[:0:-1]
                p.memset(ST, 0.0)
                for (t0, Tn, is_ctx) in chunks:
                    ci = cnt["c"]
                    cnt["c"] += 1
                    ubt, uft = ub[ci % NW], uf[ci % NW]
                    p.dma(uft[:, 0:Tn], u_d[:, b, t0:t0 + Tn])
                    p.copy(ubt[:, 0:Tn], uft[:, 0:Tn], eng="act")
                    if r == 0:
                        p.act(yacc[:, t0:t0 + Tn], uft[:, 0:Tn], AF.Copy, scale=dcol)
                    py = pY[ci % 2]
                    fwd = (r == 0)
                    tab = slice(0, Tn) if fwd else slice(Tn - 1, None, -1) if False else None
                    for g in range(8):
                        gr = r * 8 + g
                        w = cnt["w"]
                        cnt["w"] += 1
                        pa, pb = pA[w % 2], pB[w % 2]
                        p.mm(pa[:, 0:Tn], LB[:, gr, :], ubt[:, 0:Tn], start=True, stop=True)
                        p.mm(pb[:, 0:Tn], LBs[:, gr, :], ubt[:, 0:Tn], start=True, stop=True)
                        if fwd:
                            cosv = COS[:, gr, 0:Tn]
                            sinv = SIN[:, gr, 0:Tn]
                        else:
                            cosv = View(COS, COS.t[:, gr, Tn - 1::-1] if Tn - 1 >= 0 else None)
                            sinv = View(SIN, SIN.t[:, gr, Tn - 1::-1])
                        a1, a2, gt = m1[w % NW], m2[w % NW], gs[w % NW]
                        p.tt(a1[:, 0:Tn], pa[:, 0:Tn], cosv, ALU.mult)
                        p.tt(a2[:, 0:Tn], pb[:, 0:Tn], sinv, ALU.mult)
                        p.tt(a1[:, 0:Tn], a1[:, 0:Tn], a2[:, 0:Tn], ALU.add, eng="pool")
                        if fwd:
                            d1, oo = a1.t[:, 0:Tn], gt.t[:, 0:Tn]
                        else:
                            d1, oo = a1.t[:, Tn - 1::-1], gt.t[:, Tn - 1::-1]
                        d0 = rho_c.t[:, gr:gr + 1].broadcast_to([128, Tn])
                        ini = ST.t[:, gr:gr + 1]
                        p.op("dve", lambda e, oo=oo, d0=d0, d1=d1, ini=ini: e.tensor_tensor_scan(oo, d0, d1, ini, ALU.mult, ALU.add),
                             [a1, rho_c, ST], [gt])
                        g1, g2 = G1[w % NW], G2[w % NW]
                        p.tt(g1[:, 0:Tn], gt[:, 0:Tn], cosv, ALU.mult, eng="pool")
                        p.tt(g2[:, 0:Tn], gt[:, 0:Tn], sinv, ALU.mult, eng="pool")
                        p.mm(py[:, 0:Tn], L1[:, gr, :], g1[:, 0:Tn], start=(g == 0), stop=False)
                        p.mm(py[:, 0:Tn], L2[:, gr, :], g2[:, 0:Tn], start=False, stop=(g == 7))
                        last = (Tn - 1) if fwd else 0
                        Rm = R2 if is_ctx else R5
                        p.mm(pR[:, gr:gr + 1], Rm[:, gr, :], gt[:, last:last + 1], start=True, stop=True)
                        p.copy(ST[:, gr:gr + 1], pR[:, gr:gr + 1], eng="act")
                    p.tt(yacc[:, t0:t0 + Tn], yacc[:, t0:t0 + Tn], py[:, 0:Tn], ALU.add)
            p.dma(y_d[:, b, :], yacc, eng="act")
        p.emit()
    return nc


def s5_host_params(inp, slot, cb):
    gs = slice(8 * cb, 8 * cb + 8)
    lam_re = inp["s5_lam_re"][slot][:, gs].reshape(16, 64)
    lam_im = inp["s5_lam_im"][slot][:, gs].reshape(16, 64)
    ldt = inp["s5_log_dt"][slot][:, gs].reshape(16)
    lam_c = np.zeros((128, 2, 16), np.float32)
    lam_c[:, 0, :] = np.concatenate([lam_re.T, lam_re.T], 0)
    lam_c[:, 1, :] = np.concatenate([lam_im.T, lam_im.T], 0)
    ldt_c = np.ascontiguousarray(np.broadcast_to(ldt[None, :], (128, 16))).astype(np.float32)
    lam_r = np.zeros((128, 2, 1024), np.float32)
    lam_r[:, 0, :] = lam_re.reshape(1, 1024)
    lam_r[:, 1, :] = lam_im.reshape(1, 1024)
    ldt_r = np.ascontiguousarray(np.broadcast_to(np.repeat(ldt, 64)[None, :], (128, 1024))).astype(np.float32)
    b_re = inp["s5_b_re"][slot][:, gs].reshape(16, 64, 16)
    b_im = inp["s5_b_im"][slot][:, gs].reshape(16, 64, 16)
    bT = np.ascontiguousarray(np.stack([b_re.transpose(0, 2, 1), b_im.transpose(0, 2, 1)], 0)).astype(np.float32)
    c_re = inp["s5_c_re"][slot][:, gs].reshape(16, 16, 64)
    c_im = inp["s5_c_im"][slot][:, gs].reshape(16, 16, 64)
    cT = np.ascontiguousarray(np.stack([c_re.transpose(0, 2, 1), c_im.transpose(0, 2, 1)], 0)).astype(np.float32)
    d = np.ascontiguousarray(inp["s5_d"][slot][128 * cb:128 * cb + 128, None]).astype(np.float32)
    return {"lam_c": lam_c, "ldt_c": ldt_c, "lam_r": lam_r, "ldt_r": ldt_r, "bT": bT, "cT": cT, "dskip": d}

def subtile(p, parent, name, c0, n):
    return parent[:, c0:c0 + n]


def build_gdn(NLAT, NCTX, dbg=9):
    NTOK = NCTX + NLAT
    ntile = NTOK // 128
    nct = NCTX // 128
    nc = bass.Bass("TRN2", target_bir_lowering=False)
    x_d = nc.dram_tensor("qkv", [NTOK, 1536], F32, kind="ExternalInput").ap()
    ab_d = nc.dram_tensor("ab", [NTOK, 16], F32, kind="ExternalInput").ap()
    cw_d = nc.dram_tensor("cw", [3, 1536], F32, kind="ExternalInput").ap()
    al_d = nc.dram_tensor("alog", [1, 8], F32, kind="ExternalInput").ap()
    db_d = nc.dram_tensor("dtb", [1, 8], F32, kind="ExternalInput").ap()
    o_d = nc.dram_tensor("o", [2, NTOK, 512], F32, kind="ExternalOutput").ap()
    with ExitStack() as st:
        p = Prog(nc, st)
        ident = make_ident(p)
        ones = p.sb("ones", [128, 128])
        p.memset(ones, 1.0)
        BLK = p.sb("BLK", [128, 128])
        p.memset(BLK, 0.0)
        p.memset(BLK[0:64, 0:64], 1.0)
        p.memset(BLK[64:128, 64:128], 1.0)
        SELa = p.sb("SELa", [128, 128])
        SELb = p.sb("SELb", [128, 128])
        p.memset(SELa, 0.0)
        p.memset(SELa[0:64, :], 1.0)
        p.memset(SELb, 0.0)
        p.memset(SELb[64:128, :], 1.0)

        def tri(name, op, sgn):
            t = p.sb(name, [128, 128])
            ta, oa = t.t[:], ones.t[:]
            p.op("pool", lambda e: e.affine_select(ta, oa, [[-sgn, 128]], op, 0.0, base=0, channel_multiplier=sgn), [ones], [t])
            p.tt(t, t, BLK, ALU.mult, eng="pool")
            return t

        GE = tri("M_ge", ALU.is_ge, 1)
        LE = tri("M_le", ALU.is_ge, -1)
        GT_ = tri("M_gt", ALU.is_gt, 1)
        LT_ = tri("M_lt", ALU.is_gt, -1)
        TRI = [LE, GE]
        INCL = [GE, LE]
        STRICT = [GT_, LT_]
        CW = [p.sb(f"CW{k}", [128, 1536]) for k in range(3)]
        for k in range(3):
            p.dma(CW[k], cw_d[k:k + 1, :].partition_broadcast(128))
        negA = p.sb("negA", [128, 8])
        dtb = p.sb("dtb_s", [128, 8])
        p.dma(negA, al_d.partition_broadcast(128))
        p.dma(dtb, db_d.partition_broadcast(128))
        p.act(negA, negA, AF.Exp)
        p.ts(negA, negA, -1.0, None, op0=ALU.mult)
        eps6 = p.sb("eps6", [128, 1])
        p.memset(eps6, 1e-6)
        NS = 2
        Xm = [p.sb(f"Xm{i}", [128, 1536]) for i in range(NS)]
        X0 = [p.sb(f"X0{i}", [128, 1536]) for i in range(NS)]
        Xp = [p.sb(f"Xp{i}", [128, 1536]) for i in range(NS)]
        QKV = [p.sb(f"QKV{i}", [128, 1536]) for i in range(NS)]
        SQ = p.sb("SQ", [128, 1024])
        NRM = [p.sb(f"NRM{i}", [128, 8]) for i in range(NS)]
        ABt = [p.sb(f"AB{i}", [128, 16]) for i in range(NS)]
        Gt = [p.sb(f"G{i}", [128, 8]) for i in range(NS)]
        Bt = [p.sb(f"Bt{i}", [128, 8]) for i in range(NS)]
        Ot = [p.sb(f"Ot{i}", [128, 512]) for i in range(NS)]
        gcol = [p.sb(f"gcol{i}", [128, 4]) for i in range(NS)]
        egc = [p.sb(f"egc{i}", [128, 4]) for i in range(NS)]
        bkc = [p.sb(f"bkc{i}", [128, 4]) for i in range(NS)]
        kdc = [p.sb(f"kdc{i}", [128, 4]) for i in range(NS)]
        gea = [p.sb(f"gea{i}", [128, 4]) for i in range(NS)]
        geb = [p.sb(f"geb{i}", [128, 4]) for i in range(NS)]

        def U(name, shape=[128, 128]):
            return [p.sb(f"{name}{i}", shape) for i in range(NS)]

        kT, qT, Gb, Dm, egr = U("kT"), U("qT"), U("Gb"), U("Dm"), U("egr")
        Am, N0, NT0, N1, NT1 = U("Am"), U("N0_"), U("NT0_"), U("N1_"), U("NT1_")
        Xs = U("Xs", [128, 256])
        wT, qkT, qdT, kd, vn = U("wT"), U("qkT"), U("qdT"), U("kd"), U("vn")
        for i in range(NS):
            p.memset(vn[i], 0.0)
        S = [p.sb(f"S{h}", [128, 128]) for h in range(4)]
        banks = [p.ps(f"bank{i}", [128, 512]) for i in range(8)]
        pTr = [[subtile(p, banks[s], f"pTr{s}_{j}", j * 128, 128) for j in range(4)] for s in range(2)]
        pMi = [[subtile(p, banks[2 + s], f"pMi{s}_{j}", j * 128, 128) for j in range(4)] for s in range(2)]
        pY = [subtile(p, banks[4], "pY0", 0, 256), subtile(p, banks[5], "pY1", 0, 256)]
        pN = [subtile(p, banks[4], "pN0", 256, 128), subtile(p, banks[5], "pN1", 256, 128)]
        pNT = [subtile(p, banks[4], "pNT0", 384, 128), subtile(p, banks[5], "pNT1", 384, 128)]
        pW = [subtile(p, banks[6 + u], f"pW{u}", 0, 128) for u in range(2)]
        pO = [subtile(p, banks[6 + u], f"pO{u}", 128, 128) for u in range(2)]
        pS = [subtile(p, banks[6 + u], f"pS{u}", 256, 128) for u in range(2)]
        pG = subtile(p, banks[2], "pG", 384, 64)

        def phase1(ti, s):
            t0 = ti * 128
            first = (ti == 0) or (ti == nct)
            last = (ti == nct - 1) or (ti == ntile - 1)
            xm, x0, xp, qkv = Xm[s], X0[s], Xp[s], QKV[s]
            p.dma(x0, x_d[t0:t0 + 128, :])
            if first:
                p.memset(xm[0:32, :], 0.0, eng="pool")
                p.dma(xm[1:128, :], x_d[t0:t0 + 127, :])
            else:
                p.dma(xm, x_d[t0 - 1:t0 + 127, :])
            if last:
                p.memset(xp[96:128, :], 0.0, eng="pool")
                p.dma(xp[0:127, :], x_d[t0 + 1:t0 + 128, :])
            else:
                p.dma(xp, x_d[t0 + 1:t0 + 129, :])
            p.dma(ABt[s], ab_d[t0:t0 + 128, :])
            p.tt(xm, xm, CW[0], ALU.mult)
            p.tt(x0, x0, CW[1], ALU.mult, eng="pool")
            p.tt(xp, xp, CW[2], ALU.mult, eng="pool")
            p.tt(xm, xm, x0, ALU.add)
            p.tt(xm, xm, xp, ALU.add)
            p.act(qkv, xm, AF.Silu)
            p.act(SQ, qkv[:, 0:1024], AF.Square)
            nr = NRM[s]
            p.reduce(nr, SQ.v.rearrange("p (h d) -> p h d", d=128), ALU.add)
            p.act(nr, nr, AF.Sqrt, bias=eps6)
            nra = nr.t[:]
            p.op("dve", lambda e: e.reciprocal(nra, nra), [nr], [nr])
            p.ts(nr[:, 0:4], nr[:, 0:4], 128.0 ** -0.5, None, op0=ALU.mult)
            bc = View(nr, nr.t[:].unsqueeze(2).broadcast_to([128, 8, 128]))
            qk3 = qkv[:, 0:1024].rearrange("p (h d) -> p h d", d=128)
            p.tt(qk3, qk3, bc, ALU.mult)
            g, bt = Gt[s], Bt[s]
            p.tt(g, ABt[s][:, 0:8], dtb, ALU.add)
            p.act(g, g, AF.Exp)
            p.act(g, g, AF.Ln, bias=1.0)
            p.tt(g, g, negA, ALU.mult)
            p.act(bt, ABt[s][:, 8:16], AF.Sigmoid)

        def unit(s, us, r, h):
            qkv, g, bt = QKV[s], Gt[s], Bt[s]
            c = r * 4 + h
            q = qkv[:, h * 128:(h + 1) * 128]
            k = qkv[:, 512 + h * 128:512 + (h + 1) * 128]
            v = qkv[:, 1024 + h * 128:1024 + (h + 1) * 128]
            tr, mi = pTr[us], pMi[us]
            gc_, eg_, bk_, kd_ = gcol[s][:, h:h + 1], egc[s][:, h:h + 1], bkc[s][:, h:h + 1], kdc[s][:, h:h + 1]
            if dbg == 16:
                p.mm(tr[0], ones, ident, start=True, stop=True)
                p.mm(tr[1], BLK, ident, start=True, stop=True)
                p.copy(kT[us], tr[0], eng="act")
                p.copy(qT[us], tr[1], eng="act")
                return
            if dbg == 17:
                p.mm(tr[0], k, ident, start=True, stop=True)
                p.mm(tr[1], q, ident, start=True, stop=True)
                p.copy(kT[us], tr[0], eng="dve")
                p.copy(qT[us], tr[1], eng="dve")
                return
            p.mm(tr[0], k, ident, start=True, stop=True)
            p.mm(tr[1], q, ident, start=True, stop=True)
            if dbg == 10:
                return
            p.copy(kT[us], tr[0], eng="act")
            p.copy(qT[us], tr[1], eng="act")
            if dbg == 11:
                return
            yield
            p.ts(Gb[us], ones, g[:, c:c + 1], None, op0=ALU.mult, eng="pool")
            p.mm(mi[0], Gb[us], TRI[r], start=True, stop=True)
            p.ts(Dm[us], mi[0], gc_, 0.0, op0=ALU.subtract, op1=ALU.max)
            p.act(Dm[us], Dm[us], AF.Exp, scale=-1.0)
            p.tt(Dm[us], Dm[us], INCL[r], ALU.mult, eng="pool")
            p.act(egr[us], mi[0], AF.Exp)
            if dbg == 12:
                return
            yield
            p.mm(mi[1], kT[us], kT[us], start=True, stop=True)
            p.tt(Am[us], mi[1], Dm[us], ALU.mult)
            p.stt(Am[us], Am[us], bt[:, c:c + 1], STRICT[r], ALU.mult, ALU.mult)
            p.ts(N0[us], Am[us], -1.0, None, op0=ALU.mult, eng="pool")
            p.mm(tr[2], Am[us], ident, start=True, stop=True)
            p.ts(NT0[us], tr[2], -1.0, None, op0=ALU.mult)
            if dbg == 13:
                return
            yield
            p.mm(mi[2], qT[us], kT[us], start=True, stop=True)
            p.tt(Am[us], mi[2], Dm[us], ALU.mult)
            p.mm(tr[3], Am[us], ident, start=True, stop=True)
            p.copy(qkT[us], tr[3], eng="act")
            p.tt(qdT[us], qT[us], egr[us], ALU.mult, eng="pool")
            p.ts(kd[us], k, kd_, None, op0=ALU.mult, eng="pool")
            if dbg == 14:
                return
            yield
            X = Xs[us]
            p.ts(X[:, 0:128], v, bt[:, c:c + 1], None, op0=ALU.mult)
            p.ts(X[:, 128:256], k, bk_, None, op0=ALU.mult)
            Ncur, NTcur = N0[us], NT0[us]
            Nnxt, NTnxt = N1[us], NT1[us]
            for lv in range(6):
                py = pY[us]
                p.mm(py, NTcur, X, start=True, stop=True)
                if lv < 5:
                    p.mm(pN[us], NTcur, Ncur, start=True, stop=True)
                    p.mm(pNT[us], Ncur, NTcur, start=True, stop=True)
                p.tt(X, X, py, ALU.add)
                if lv < 5:
                    p.copy(Nnxt, pN[us], eng="act")
                    p.copy(NTnxt, pNT[us], eng="act")
                    Ncur, Nnxt = Nnxt, Ncur
                    NTcur, NTnxt = NTnxt, NTcur
                yield
            if dbg == 15:
                return
            p.mm(tr[0], X[:, 128:256], ident, start=True, stop=True)
            p.copy(wT[us], tr[0], eng="act")

        def steps(s, us, r, h, o_t):
            X = Xs[us]
            order = [0, 1] if r == 0 else [1, 0]
            for cidx in order:
                rows = slice(64 * cidx, 64 * cidx + 64)
                ge = (gea if cidx == 0 else geb)[s][:, h:h + 1]
                p.mm(pW[us][rows, :], wT[us][:, rows], S[h], start=True, stop=True)
                p.tt(vn[us][rows, :], X[rows, 0:128], pW[us][rows, :], ALU.subtract)
                yield
                p.mm(pO[us][rows, :], qdT[us][:, rows], S[h], start=True, stop=False)
                p.mm(pO[us][rows, :], qkT[us][:, rows], vn[us], start=False, stop=True)
                p.mm(pS[us], kd[us][rows, :], vn[us][rows, :], start=True, stop=True)
                p.copy(o_t[rows, h * 128:(h + 1) * 128], pO[us][rows, :], eng="act")
                p.stt(S[h], S[h], ge, pS[us], ALU.mult, ALU.add)
                yield

        def rr_(gens):
            alive = True
            while alive:
                alive = False
                for gn in gens:
                    try:
                        next(gn)
                        alive = True
                    except StopIteration:
                        pass

        ucnt = {"u": 0}
        for r in range(2):
            segs = [list(range(0, nct)), list(range(nct, ntile))]
            if r == 1:
                segs = [s_[::-1] for s_ in segs]
            for h in range(4):
                p.memset(S[h], 0.0)
            seq = segs[0] + segs[1]
            for n_, ti in enumerate(seq):
                s = n_ % NS
                if dbg == 19:
                    p.memset(QKV[s], 1.0)
                    for h in range(4):
                        us = ucnt["u"] % NS
                        ucnt["u"] += 1
                        p.mm(pTr[us][0], ones, ident, start=True, stop=True)
                        p.mm(pTr[us][1], BLK, ident, start=True, stop=True)
                        p.copy(kT[us], pTr[us][0], eng="act")
                        p.copy(qT[us], pTr[us][1], eng="act")
                    p.copy(Ot[s], QKV[s][:, 0:512])
                    p.dma(o_d[r, ti * 128:(ti + 1) * 128, :], Ot[s], eng="act")
                    continue
                phase1(ti, s)
                g = Gt[s]
                if dbg < 1:
                    p.copy(Ot[s], QKV[s][:, 0:512])
                    p.dma(o_d[r, ti * 128:(ti + 1) * 128, :], Ot[s], eng="act")
                    continue
                gsl = g[:, r * 4:r * 4 + 4]
                if dbg == 18:
                    for h in range(4):
                        us = ucnt["u"] % NS
                        ucnt["u"] += 1
                        p.mm(pTr[us][0], ones, ident, start=True, stop=True)
                        p.mm(pTr[us][1], BLK, ident, start=True, stop=True)
                        p.copy(kT[us], pTr[us][0], eng="act")
                        p.copy(qT[us], pTr[us][1], eng="act")
                    p.copy(Ot[s], QKV[s][:, 0:512])
                    p.dma(o_d[r, ti * 128:(ti + 1) * 128, :], Ot[s], eng="act")
                    continue
                p.mm(pG[:, 0:4], TRI[r], gsl, start=True, stop=True)
                p.mm(pG[:, 4:8], BLK, gsl, start=True, stop=True)
                p.mm(pG[:, 8:12], SELa, gsl, start=True, stop=True)
                p.mm(pG[:, 12:16], SELb, gsl, start=True, stop=True)
                p.copy(gcol[s], pG[:, 0:4])
                p.act(egc[s], pG[:, 0:4], AF.Exp)
                p.tt(bkc[s], egc[s], Bt[s][:, r * 4:r * 4 + 4], ALU.mult)
                p.tt(kdc[s], pG[:, 4:8], gcol[s], ALU.subtract)
                p.act(kdc[s], kdc[s], AF.Exp)
                p.act(gea[s], pG[:, 8:12], AF.Exp)
                p.act(geb[s], pG[:, 12:16], AF.Exp)
                for hp in (0, 2):
                    if dbg >= 2:
                        rr_([unit(s, 0, r, hp), unit(s, 1, r, hp + 1)])
                    if dbg in (3, 9):
                        rr_([steps(s, 0, r, hp, Ot[s]), steps(s, 1, r, hp + 1, Ot[s])])
                if dbg not in (3, 9):
                    p.copy(Ot[s], QKV[s][:, 0:512])
                p.dma(o_d[r, ti * 128:(ti + 1) * 128, :], Ot[s], eng="act")
        p.emit()
    return nc

NLAT_FULL, NCTX_FULL, BATCH = 8192, 256, 4
NL_CORE, NC_CORE = 4096, 128


def build_T(specs, NL=None, NC=NC_CORE):
    NL = NL_CORE if NL is None else NL
    nc = bass.Bass("TRN2", target_bir_lowering=False)
    NT = NL + NC

    def din(name, shape, dt=F32):
        return nc.dram_tensor(name, list(shape), dt, kind="ExternalInput").ap()

    def dout(name, shape, dt=F32):
        return nc.dram_tensor(name, list(shape), dt, kind="ExternalOutput").ap()

    x_in = din("x_in", [NT, D])
    x_out = dout("x_out", [NT, D])
    nlay = 1 + max(s[1] for s in specs)
    modl = [din(f"modl{i}", [9, D]) for i in range(nlay)]
    modc = [din(f"modc{i}", [9, D]) for i in range(nlay)]
    gs = [din(f"g{i}", [3, D]) for i in range(nlay)]
    with ExitStack() as st:
        p = Prog(nc, st)
        T = TCtx(p, nc)
        cur = x_in
        for s in specs:
            kind, lay = s[0], s[1]
            ml, mc, g = modl[lay], modc[lay], gs[lay]
            if kind == "ffn":
                j, tag = s[2], s[3]
                wi = din(f"wi{tag}", [D, 2 * DFF])
                wo = din(f"wo{tag}", [DFF, D])
                ffn_pass(T, cur, x_out, NL, NC, wi, wo, ml, mc, g[(0 if j == 0 else 2):(1 if j == 0 else 3), :], j)
                cur = x_out
            elif kind == "s5pre":
                u = dout("u_out", [NT, D])
                s5pre_pass(T, cur, u, NL, NC, ml, mc, g[1:2, :])
            elif kind == "s5post":
                yT = din("yT", [D, NT])
                wg = din("wglu", [D, 2 * D])
                s5post_pass(T, cur, x_out, NL, NC, ml, mc, g[1:2, :], yT, wg)
                cur = x_out
            elif kind == "attpre":
                wq = din("wqkv", [D, 1536])
                qg = din("qg", [1, 128])
                kg = din("kg", [1, 128])
                rc = din("ropeC", [NL, 128])
                rs_ = din("ropeS", [NL, 128])
                qkv = dout("qkv_out", [NT, 1536], BF16)
                attpre_pass(T, cur, qkv, NL, NC, ml, mc, g[1:2, :], wq, qg, kg, rc, rs_)
            elif kind == "attpost":
                aT = din("aT", [D, NT], BF16)
                w = din("w_o", [D, D])
                proj_resid_pass(T, cur, x_out, NL, NC, ml, mc, g[1:2, :], aT, w)
                cur = x_out
            elif kind == "gdnpre":
                w = din("w_in", [D, 4128])
                pr = dout("proj_out", [NT, 4128])
                gdnpre_pass(T, cur, pr, NL, NC, ml, mc, g[1:2, :], w)
            elif kind == "gdnpost":
                o2 = din("o2", [2, NT, D])
                z = din("z", [NT, D])
                og = din("og", [1, 128])
                w = din("gw_o", [D, D])
                gdnpost_pass(T, cur, x_out, NL, NC, ml, mc, g[1:2, :], o2, z, og, w)
                cur = x_out
            else:
                raise ValueError(kind)
        p.emit()
    return nc


def to_cores(al, ac):
    outs = []
    for c in range(8):
        b, hf = c // 2, c % 2
        outs.append(np.ascontiguousarray(np.concatenate(
            [al[b, hf * NL_CORE:(hf + 1) * NL_CORE], ac[b, hf * NC_CORE:(hf + 1) * NC_CORE]], 0)))
    return outs


def from_cores(arrs):
    F_ = arrs[0].shape[-1]
    al = np.zeros((BATCH, NLAT_FULL, F_), arrs[0].dtype)
    ac = np.zeros((BATCH, NCTX_FULL, F_), arrs[0].dtype)
    for c in range(8):
        b, hf = c // 2, c % 2
        al[b, hf * NL_CORE:(hf + 1) * NL_CORE] = arrs[c][:NL_CORE]
        ac[b, hf * NC_CORE:(hf + 1) * NC_CORE] = arrs[c][NL_CORE:]
    return al, ac


def featmajor_to_cores(full):
    outs = []
    for c in range(8):
        b, hf = c // 2, c % 2
        f = full[b]
        outs.append(np.ascontiguousarray(np.concatenate(
            [f[:, hf * NL_CORE:(hf + 1) * NL_CORE], f[:, NLAT_FULL + hf * NC_CORE:NLAT_FULL + (hf + 1) * NC_CORE]], 1)))
    return outs


def run(nc, in_maps):
    res = run_bass_kernel_spmd(nc, in_maps, core_ids=list(range(8)))
    return res.results


def kernel(**inp):
    global NLAT_FULL, NL_CORE
    inp = {k: np.asarray(v) for k, v in inp.items()}
    NLAT_FULL = inp["x"].shape[1]
    NL_CORE = NLAT_FULL // 2
    mod = run_mod(inp)
    ropeC, ropeS = rope_tables(NLAT_FULL)

    def mods(core, layers):
        b = core // 2
        d = {}
        for i, l in enumerate(layers):
            d[f"modl{i}"] = np.ascontiguousarray(mod[l, b])
            d[f"modc{i}"] = np.ascontiguousarray(mod[l, 4])
            d[f"g{i}"] = np.ascontiguousarray(inp["norm_g"][l])
        return d

    def ffw(tag, l, s):
        return {f"wi{tag}": inp["ffn_wi"][l, s], f"wo{tag}": inp["ffn_wo"][l, s]}

    xs = to_cores(inp["x"], inp["ctx"])

    def s5_core(us, slot):
        ul, uc = from_cores(us)
        maps = []
        for cb in range(8):
            sl = slice(128 * cb, 128 * cb + 128)
            uT = np.concatenate([uc[:, :, sl], ul[:, :, sl]], 1).transpose(2, 0, 1)
            m = {"uT": np.ascontiguousarray(uT)}
            m.update(s5_host_params(inp, slot, cb))
            maps.append(m)
        res = run(build_s5(NLAT_FULL, NCTX_FULL, BATCH), maps)
        full = []
        for b in range(BATCH):
            yb = np.concatenate([res[cb]["yT"][:, b, :] for cb in range(8)], 0)
            full.append(np.concatenate([yb[:, NCTX_FULL:], yb[:, :NCTX_FULL]], 1))
        return featmajor_to_cores(full)

    nc = build_T([("ffn", 0, 0, "a"), ("s5pre", 0)])
    maps = [dict(x_in=xs[c], **mods(c, [0]), **ffw("a", 0, 0)) for c in range(8)]
    res = run(nc, maps)
    xs = [r["x_out"] for r in res]
    yTs = s5_core([r["u_out"] for r in res], 0)
    nc = build_T([("s5post", 0), ("ffn", 0, 2, "a"), ("ffn", 1, 0, "b"), ("attpre", 1)])
    maps = []
    for c in range(8):
        hf = c % 2
        m = dict(x_in=xs[c], yT=yTs[c], wglu=inp["s5_w_glu"][0], **mods(c, [0, 1]), **ffw("a", 0, 1), **ffw("b", 1, 0))
        m.update(wqkv=inp["attn_w_qkv"][0], qg=inp["attn_q_gain"][0:1], kg=inp["attn_k_gain"][0:1],
                 ropeC=np.ascontiguousarray(ropeC[hf * NL_CORE:(hf + 1) * NL_CORE]),
                 ropeS=np.ascontiguousarray(ropeS[hf * NL_CORE:(hf + 1) * NL_CORE]))
        maps.append(m)
    res = run(nc, maps)
    xs = [r["x_out"] for r in res]
    ql, qc = from_cores([r["qkv_out"] for r in res])
    maps = []
    for c in range(8):
        b, kvh = c // 2, c % 2
        qT = np.stack([np.concatenate([ql[b][:, (kvh * 4 + g) * 128:(kvh * 4 + g + 1) * 128],
                                       qc[b][:, (kvh * 4 + g) * 128:(kvh * 4 + g + 1) * 128]], 0).T for g in range(4)], 0)
        ks = slice(1024 + kvh * 128, 1024 + (kvh + 1) * 128)
        vs = slice(1280 + kvh * 128, 1280 + (kvh + 1) * 128)
        kT = np.concatenate([qc[b][:, ks], ql[b][:, ks]], 0).T
        v = np.concatenate([qc[b][:, vs], ql[b][:, vs]], 0)
        maps.append({"qT": np.ascontiguousarray(qT), "kT": np.ascontiguousarray(kT), "v": np.ascontiguousarray(v),
                     "qg": inp["attn_q_gain"][0:1], "kg": inp["attn_k_gain"][0:1]})
    res = run(build_att(NLAT_FULL, NCTX_FULL), maps)
    full = []
    for b in range(BATCH):
        full.append(np.concatenate([res[2 * b + kvh]["oT"][g] for kvh in range(2) for g in range(4)], 0))
    aTs = featmajor_to_cores(full)
    nc = build_T([("attpost", 0), ("ffn", 0, 2, "a"), ("ffn", 1, 0, "b"), ("gdnpre", 1)])
    maps = [dict(x_in=xs[c], aT=aTs[c], w_o=inp["attn_w_o"][0], w_in=inp["gdn_w_in"][0], **mods(c, [1, 2]),
                 **ffw("a", 1, 1), **ffw("b", 2, 0)) for c in range(8)]
    res = run(nc, maps)
    xs = [r["x_out"] for r in res]
    projs = [r["proj_out"] for r in res]
    pl, pc = from_cores(projs)
    def sel3(a, hh):
        hs = slice(512 * hh, 512 * hh + 512)
        return np.concatenate([a[..., 0:1024][..., hs], a[..., 1024:2048][..., hs], a[..., 2048:3072][..., hs]], -1)

    maps = []
    for c in range(8):
        b, hh = c // 2, c % 2
        P = np.concatenate([pc[b], pl[b]], 0)
        ab = P[:, 4096:].reshape(-1, 2, 2, 8)[:, :, :, 4 * hh:4 * hh + 4].reshape(-1, 16)
        maps.append({"qkv": np.ascontiguousarray(sel3(P, hh)), "ab": np.ascontiguousarray(ab),
                     "cw": np.ascontiguousarray(sel3(inp["gdn_conv_w"][0], hh)),
                     "alog": np.ascontiguousarray(inp["gdn_a_log"][0][:, 4 * hh:4 * hh + 4].reshape(1, 8)),
                     "dtb": np.ascontiguousarray(inp["gdn_dt_bias"][0][:, 4 * hh:4 * hh + 4].reshape(1, 8))})
    res = run(build_gdn(NLAT_FULL, NCTX_FULL), maps)
    o2s = []
    for c in range(8):
        b, hf = c // 2, c % 2
        o = np.concatenate([res[2 * b]["o"], res[2 * b + 1]["o"]], -1)
        lat = o[:, NCTX_FULL + hf * NL_CORE:NCTX_FULL + (hf + 1) * NL_CORE]
        cx = o[:, hf * NC_CORE:(hf + 1) * NC_CORE]
        o2s.append(np.ascontiguousarray(np.concatenate([lat, cx], 1)))
    nc = build_T([("gdnpost", 0), ("ffn", 0, 2, "a"), ("ffn", 1, 0, "b"), ("s5pre", 1)])
    maps = [dict(x_in=xs[c], o2=o2s[c], z=np.ascontiguousarray(projs[c][:, 3072:4096]), og=inp["gdn_o_gain"][0:1],
                 gw_o=inp["gdn_w_o"][0], **mods(c, [2, 3]), **ffw("a", 2, 1), **ffw("b", 3, 0)) for c in range(8)]
    res = run(nc, maps)
    xs = [r["x_out"] for r in res]
    yTs = s5_core([r["u_out"] for r in res], 1)
    nc = build_T([("s5post", 0), ("ffn", 0, 2, "a")])
    maps = [dict(x_in=xs[c], yT=yTs[c], wglu=inp["s5_w_glu"][1], **mods(c, [3]), **ffw("a", 3, 1)) for c in range(8)]
    res = run(nc, maps)
    xl, _ = from_cores([r["x_out"] for r in res])
    return xl.astype(np.float32)
```
